# Optimizing a Trainium2 kernel written in Bass

```python
import jax, jax.numpy as jnp
from jax import lax
import numpy as np

D_MODEL = 1024
BATCH = 8
SEQ = 4096
DEPTH = 1

ATT_HEAD_DIM = 64
ATT_HEADS_PER_GROUP = 8
DILATED_GROUPS = ((128, 1), (512, 4), (2048, 16))
N_ATT_GROUPS = len(DILATED_GROUPS)
ATT_WIDTH = N_ATT_GROUPS * ATT_HEADS_PER_GROUP * ATT_HEAD_DIM
ATT_OUT_WIDTH = ATT_HEADS_PER_GROUP * ATT_HEAD_DIM
BAND_BLOCK = 128
ROPE_THETA = 10000.0

RET_HEADS = 4
RET_QK_WIDTH = D_MODEL // 2
RET_V_WIDTH = D_MODEL
RET_KEY_DIM = RET_QK_WIDTH // RET_HEADS
RET_VALUE_DIM = RET_V_WIDTH // RET_HEADS
RET_CHUNK = 128

FFN_HIDDEN = ((8 * D_MODEL // 3 + 255) // 256) * 256
NORM_EPS = 1e-6

IN_SPLITS = (ATT_WIDTH, ATT_WIDTH, ATT_WIDTH,
             RET_QK_WIDTH, RET_QK_WIDTH, RET_V_WIDTH, RET_V_WIDTH,
             D_MODEL, D_MODEL)
IN_WIDTH = int(sum(IN_SPLITS))
IN_OFFSETS = tuple(int(o) for o in np.cumsum(IN_SPLITS)[:-1])

kernel_name = "hybrid_dilated_attn_retention_gated"


def rmsnorm(x, g):
    xf = x.astype(jnp.float32)
    y = xf * lax.rsqrt(jnp.mean(xf * xf, axis=-1, keepdims=True) + NORM_EPS)
    return (y * g.astype(jnp.float32)).astype(x.dtype)


def apply_rope(t, pos):
    hd = t.shape[-1]
    inv = ROPE_THETA ** (-jnp.arange(0, hd, 2, dtype=jnp.float32) / hd)
    ang = pos[:, None] * inv[None, :]
    c = jnp.cos(ang)[:, None, :]
    s = jnp.sin(ang)[:, None, :]
    tf = t.astype(jnp.float32)
    t1, t2 = tf[..., : hd // 2], tf[..., hd // 2:]
    out = jnp.concatenate([t1 * c - t2 * s, t2 * c + t1 * s], axis=-1)
    return out.astype(t.dtype)


def dilated_causal_group(q, k, v, window, dilation):
    B, S, H, hd = q.shape
    n_strides = window // dilation
    L = S // dilation
    nb = -(-L // BAND_BLOCK)
    Lp = nb * BAND_BLOCK

    def to_sub(t):
        t = t.reshape(B, L, dilation, H, hd).transpose(0, 2, 3, 1, 4)
        return jnp.pad(t, ((0, 0), (0, 0), (0, 0), (0, Lp - L), (0, 0)))

    qs, ks, vs = to_sub(q), to_sub(k), to_sub(v)
    qb = qs.reshape(B, dilation, H, nb, BAND_BLOCK, hd)

    def band(t):
        tp = jnp.pad(t, ((0, 0), (0, 0), (0, 0), (BAND_BLOCK, 0), (0, 0)))
        prev = tp[:, :, :, :Lp].reshape(B, dilation, H, nb, BAND_BLOCK, hd)
        cur = t.reshape(B, dilation, H, nb, BAND_BLOCK, hd)
        return jnp.concatenate([prev, cur], axis=4)

    kb, vb = band(ks), band(vs)
    scores = jnp.einsum('bdhnqc,bdhnkc->bdhnqk', qb, kb).astype(jnp.float32) * (hd ** -0.5)
    qi = jnp.arange(BAND_BLOCK)[:, None]
    kj = jnp.arange(2 * BAND_BLOCK)[None, :]
    dist = BAND_BLOCK + qi - kj
    key_idx = jnp.arange(nb)[:, None, None] * BAND_BLOCK + kj[None] - BAND_BLOCK
    mask = (dist >= 0)[None] & (dist <= n_strides)[None] & (key_idx >= 0)
    scores = jnp.where(mask, scores, jnp.float32(-1e30))
    m = jnp.max(scores, axis=-1, keepdims=True)
    p = jnp.exp(scores - m)
    l = jnp.sum(p, axis=-1, keepdims=True)
    o = jnp.einsum('bdhnqk,bdhnkc->bdhnqc', (p / l).astype(v.dtype), vb)
    lse = (m + jnp.log(l))[..., 0]
    o = o.reshape(B, dilation, H, Lp, hd)[:, :, :, :L].transpose(0, 3, 1, 2, 4).reshape(B, S, H, hd)
    lse = lse.reshape(B, dilation, H, Lp)[:, :, :, :L].transpose(0, 3, 1, 2).reshape(B, S, H)
    return o, lse


def dilated_attention(q, k, v, pos):
    B, S, _ = q.shape
    shp = (B, S, N_ATT_GROUPS, ATT_HEADS_PER_GROUP, ATT_HEAD_DIM)
    q = apply_rope(q.reshape(B, S, -1, ATT_HEAD_DIM), pos).reshape(shp)
    k = apply_rope(k.reshape(B, S, -1, ATT_HEAD_DIM), pos).reshape(shp)
    v = v.reshape(shp)
    outs, lses = [], []
    for g, (window, dilation) in enumerate(DILATED_GROUPS):
        o, lse = dilated_causal_group(q[:, :, g], k[:, :, g], v[:, :, g], window, dilation)
        outs.append(o)
        lses.append(lse)
    w = jax.nn.softmax(jnp.stack(lses, axis=0), axis=0)
    o = sum(w[g][..., None].astype(outs[g].dtype) * outs[g] for g in range(N_ATT_GROUPS))
    return o.reshape(B, S, ATT_OUT_WIDTH)


def retnet_theta_shift(t, pos):
    dk = t.shape[-1]
    ang_base = 1.0 / (ROPE_THETA ** jnp.linspace(0.0, 1.0, dk // 2, dtype=jnp.float32))
    ang = pos[:, None] * ang_base[None, :]
    c = jnp.cos(ang)[:, None, :]
    s = jnp.sin(ang)[:, None, :]
    t0, t1 = t[..., 0::2], t[..., 1::2]
    r0 = t0 * c - t1 * s
    r1 = t1 * c + t0 * s
    return jnp.stack([r0, r1], axis=-1).reshape(t.shape)


def retention(q, k, v, pos):
    B, S, _ = q.shape
    C = RET_CHUNK
    nc = S // C
    q = retnet_theta_shift(q.astype(jnp.float32).reshape(B, S, RET_HEADS, RET_KEY_DIM), pos)
    k = retnet_theta_shift(k.astype(jnp.float32).reshape(B, S, RET_HEADS, RET_KEY_DIM), pos)
    k = k * (RET_KEY_DIM ** -0.5)
    v = v.astype(jnp.float32).reshape(B, S, RET_HEADS, RET_VALUE_DIM)

    def chunks(t):
        return t.reshape(B, nc, C, RET_HEADS, t.shape[-1]).transpose(0, 3, 1, 2, 4)

    qc, kc, vc = chunks(q), chunks(k), chunks(v)
    log_g = jnp.log1p(-(2.0 ** (-5.0 - jnp.arange(RET_HEADS, dtype=jnp.float32))))
    idx = jnp.arange(C, dtype=jnp.float32)
    diff = idx[:, None] - idx[None, :]
    decay = jnp.where(diff[None] >= 0, jnp.exp(jnp.maximum(diff, 0.0)[None] * log_g[:, None, None]), 0.0)
    inner = jnp.einsum('bhnid,bhnjd->bhnij', qc, kc) * decay[None, :, None]
    inner = jnp.einsum('bhnij,bhnje->bhnie', inner, vc)
    zeta = jnp.exp((C - 1 - idx)[None, :] * log_g[:, None])
    xi = jnp.exp((idx + 1.0)[None, :] * log_g[:, None])
    kv = jnp.einsum('bhncd,bhnce->bhnde', kc * zeta[None, :, None, :, None], vc)
    chunk_decay = jnp.exp(C * log_g)

    def step(R, kv_c):
        return R * chunk_decay[None, :, None, None] + kv_c, R

    R0 = jnp.zeros((B, RET_HEADS, RET_KEY_DIM, RET_VALUE_DIM), jnp.float32)
    _, R_prev = lax.scan(step, R0, kv.transpose(2, 0, 1, 3, 4))
    cross = jnp.einsum('bhncd,nbhde->bhnce', qc * xi[None, :, None, :, None], R_prev)
    o = (inner + cross).transpose(0, 2, 3, 1, 4).reshape(B, S, RET_HEADS, RET_VALUE_DIM)
    mu = jnp.mean(o, axis=-1, keepdims=True)
    var = jnp.mean(jnp.square(o - mu), axis=-1, keepdims=True)
    o = (o - mu) * lax.rsqrt(var + NORM_EPS)
    return o.reshape(B, S, RET_V_WIDTH)


def setup_inputs(seed: int = 0) -> dict:
    key = jax.random.key(seed)
    ks = jax.random.split(key, 12)
    f = jnp.float32

    def w(k, shape, fan_in):
        return jax.random.normal(k, shape, f) * (fan_in ** -0.5)

    def gain(k, shape):
        return 1.0 + 0.02 * jax.random.normal(k, shape, f)

    return {
        "x": jax.random.normal(ks[0], (BATCH, SEQ, D_MODEL), f),
        "norm_mix_g": gain(ks[1], (DEPTH, D_MODEL)),
        "w_in": w(ks[2], (DEPTH, D_MODEL, IN_WIDTH), D_MODEL),
        "w_out_attn": w(ks[3], (DEPTH, ATT_OUT_WIDTH, D_MODEL), ATT_OUT_WIDTH),
        "w_out_ret": w(ks[4], (DEPTH, RET_V_WIDTH, D_MODEL), RET_V_WIDTH),
        "w_out": w(ks[5], (DEPTH, D_MODEL, D_MODEL), D_MODEL),
        "norm_ffn_g": gain(ks[6], (DEPTH, D_MODEL)),
        "w_ffn_gate": w(ks[7], (DEPTH, D_MODEL, FFN_HIDDEN), D_MODEL),
        "w_ffn_up": w(ks[8], (DEPTH, D_MODEL, FFN_HIDDEN), D_MODEL),
        "w_ffn_down": w(ks[9], (DEPTH, FFN_HIDDEN, D_MODEL), FFN_HIDDEN),
        "norm_final_g": gain(ks[10], (D_MODEL,)),
    }


def reference(x, norm_mix_g, w_in, w_out_attn, w_out_ret, w_out, norm_ffn_g,
              w_ffn_gate, w_ffn_up, w_ffn_down, norm_final_g):
    S = x.shape[1]
    pos = jnp.arange(S, dtype=jnp.float32)
    for layer in range(DEPTH):
        h = rmsnorm(x, norm_mix_g[layer])
        proj = h @ w_in[layer]
        (qa, ka, va, qr, kr, vr, gr, gate_a, gate_r) = jnp.split(proj, IN_OFFSETS, axis=-1)
        ya = dilated_attention(qa, ka, va, pos) @ w_out_attn[layer]
        yr = retention(qr, kr, vr, pos).astype(x.dtype) * jax.nn.silu(gr)
        yr = yr @ w_out_ret[layer]
        merged = jax.nn.sigmoid(gate_a) * ya + jax.nn.sigmoid(gate_r) * yr
        x = x + merged @ w_out[layer]
        h2 = rmsnorm(x, norm_ffn_g[layer])
        x = x + (jax.nn.silu(h2 @ w_ffn_gate[layer]) * (h2 @ w_ffn_up[layer])) @ w_ffn_down[layer]
    return rmsnorm(x, norm_final_g)
```

```python
import numpy as np
import concourse.bass as bass
import concourse.mybir as mybir
from concourse.bass_utils import run_bass_kernel_spmd

F32 = mybir.dt.float32
BF = mybir.dt.bfloat16
AF = mybir.ActivationFunctionType
ALU = mybir.AluOpType

S_LEN = 4096
D = 1024
NCH = 32
NTT = 8
FFN = 2816
NJ = FFN // 128
EPS = 1e-6
GROUP_DIL = (1, 4, 16)
NEG = -30000.0
RET_SCALE = 128.0 ** -0.5
import os
CSTAGE = int(os.environ.get('CSTAGE', '9'))

CB_ID, CB_ROTA, CB_ROTR, CB_MASK, CB_ONE0, CB_ONE1, CB_W = 0, 128, 256, 384, 640, 768, 896
CF_DEC, CF_ZETA, CF_XI, CF_W = 0, 512, 516, 1028


def _phase(name, phases):
    import contextlib
    if name in phases:
        with contextlib.ExitStack() as st:
            yield st


class _Op:
    __slots__ = ("idx", "eng", "fn", "dma", "deps_eng", "deps_dma", "needs_inc",
                 "ticket", "sem", "semval", "slot_prev")


class Sched:
    NSLOT = 8

    def __init__(self, nc, eng_sems, dma_sems):
        self.nc = nc
        self.eng_sems = eng_sems
        self.dma_sems = dma_sems
        self.counter = {e: 0 for e in eng_sems}
        self.dma_count = {q: 0 for q in dma_sems}
        self.slot_cnt = {q: [0] * self.NSLOT for q in dma_sems}
        self.slot_last = {q: [None] * self.NSLOT for q in dma_sems}
        self.known = {e: {} for e in ("sp", "act", "dve", "pool", "pe")}
        self.begin()

    def begin(self):
        self.ops = []
        self.res = {}
        self.last = {}
        self.dmas = []

    def add(self, eng, fn, reads=(), writes=(), dma=False):
        o = _Op()
        o.idx = len(self.ops)
        o.eng = eng
        o.fn = fn
        o.dma = dma
        o.needs_inc = False
        o.ticket = None
        o.slot_prev = None
        de, dd = {}, set()

        def take(d_eng, d_dma):
            for e, i in d_eng.items():
                if de.get(e, -1) < i:
                    de[e] = i
            dd.update(d_dma)

        for r in reads:
            st = self.res.get(r)
            if st is not None:
                take(st[0], st[1])
        newep = []
        for w in writes:
            st = self.res.get(w)
            if st is not None and (st[2] or st[3]):
                take(st[2], st[3])
                newep.append(w)
        for w in writes:
            st = self.res.get(w)
            if st is None or w in newep:
                st = self.res[w] = [{}, [], {}, []]
            if dma:
                st[1].append(o.idx)
            else:
                st[0][eng] = o.idx
        for r in reads:
            if r in writes:
                continue
            st = self.res.get(r)
            if st is None:
                st = self.res[r] = [{}, [], {}, []]
            if dma:
                st[3].append(o.idx)
            else:
                st[2][eng] = o.idx
        if dma:
            q = eng
            k = self.dma_count[q] % self.NSLOT
            self.dma_count[q] += 1
            self.slot_cnt[q][k] += 1
            o.sem = self.dma_sems[q][k]
            o.semval = 16 * self.slot_cnt[q][k]
            o.slot_prev = self.slot_last[q][k]
            self.slot_last[q][k] = (o.sem, o.semval)
            self.dmas.append(o.idx)
        else:
            self.last[eng] = o.idx
        o.deps_eng = de
        o.deps_dma = dd
        for e, i in de.items():
            if not (eng == "pe" and e == "pe"):
                self.ops[i].needs_inc = True
        self.ops.append(o)
        return o

    def barrier(self):
        last = dict(self.last)
        dmas = list(self.dmas)
        for e in ("sp", "act", "dve", "pool", "pe"):
            o = self.add(e, None)
            o.deps_eng = dict(last)
            o.deps_dma = set(dmas)
            for i in last.values():
                self.ops[i].needs_inc = True

    def emit(self):
        nc = self.nc
        for o in self.ops:
            if not o.dma and o.needs_inc:
                self.counter[o.eng] += 1
                o.ticket = self.counter[o.eng]
        ops = self.ops

        def run(engname, e):
            known = self.known[engname]

            def need(sem, val):
                key = id(sem)
                if known.get(key, 0) >= val:
                    return
                e.wait_ge(sem, val)
                known[key] = val

            for o in ops:
                if o.eng != engname:
                    continue
                for en, i in o.deps_eng.items():
                    if engname == "pe" and en == "pe" and o.fn is not None:
                        continue
                    need(self.eng_sems[en], ops[i].ticket)
                for i in o.deps_dma:
                    need(ops[i].sem, ops[i].semval)
                if o.dma and o.slot_prev is not None:
                    need(*o.slot_prev)
                if o.fn is None:
                    continue
                inst = o.fn(e)
                if o.dma:
                    inst.then_inc(o.sem, 16)
                elif o.needs_inc:
                    inst.then_inc(self.eng_sems[engname], 1)

        with nc.Block() as block:
            @block.sync
            def _(e):
                run("sp", e)

            @block.scalar
            def _(e):
                run("act", e)

            @block.vector
            def _(e):
                run("dve", e)

            @block.gpsimd
            def _(e):
                run("pool", e)

            @block.tensor
            def _(e):
                run("pe", e)
        self.begin()


def build_program(debug=False, phases="ABCDE"):
    nc = bass.Bass("TRN2", target_bir_lowering=False)

    def din(name, shape, dt=F32):
        return nc.dram_tensor(name, list(shape), dt, kind="ExternalInput").ap()

    def dscr(name, shape, dt):
        kind = "ExternalOutput" if (debug and name in debug) else "Internal"
        return nc.dram_tensor(name, list(shape), dt, kind=kind).ap()

    x = din("x", [S_LEN, D])
    w_in = din("w_in", [D, 9728])
    w_oa = din("w_out_attn", [512, D])
    w_or = din("w_out_ret", [D, D])
    w_o = din("w_out", [D, D])
    w_g = din("w_ffn_gate", [D, FFN])
    w_u = din("w_ffn_up", [D, FFN])
    w_d = din("w_ffn_down", [FFN, D])
    g1Td = din("g1T", [128, 8])
    g2d = din("g2rep", [128, D])
    gFd = din("gFrep", [128, D])
    cosAd = din("cosA", [128, S_LEN])
    sinAd = din("sinA", [128, S_LEN])
    cosRd = din("cosR", [128, S_LEN])
    sinRd = din("sinR", [128, S_LEN])
    cbd = din("cb", [128, CB_W])
    cfd = din("cf", [128, CF_W])
    out = nc.dram_tensor("out", [S_LEN, D], F32, kind="ExternalOutput").ap()

    qkA = dscr("qkA", [24, 128, S_LEN], BF)
    vA = dscr("vA", [S_LEN, 1536], BF)
    qkR = dscr("qkR", [8, 128, S_LEN], BF)
    kRt = dscr("kRt", [S_LEN, 512], BF)
    vR = dscr("vR", [S_LEN, 1024], BF)
    grT = dscr("grT", [8, 128, S_LEN], BF)
    gaT = dscr("gaT", [8, 128, S_LEN], BF)
    ggT = dscr("ggT", [8, 128, S_LEN], BF)
    retT = dscr("retT", [8, 128, S_LEN], BF)
    x1d = dscr("x1d", [S_LEN, D], F32)
    h2T = dscr("h2T", [8, 128, S_LEN], BF)
    oTd = dscr("oTd", [4, 128, S_LEN], BF)

    import contextlib
    top = contextlib.ExitStack()

    def sb(stack, name, shape, dt):
        return stack.enter_context(nc.sbuf_tensor(name, list(shape), dt))

    with top:
        eng_sems = {e: top.enter_context(nc.semaphore("s_" + e)) for e in ("act", "dve", "pool", "pe")}
        dma_sems = {q: [top.enter_context(nc.semaphore(f"d_{q}{k}")) for k in range(Sched.NSLOT)]
                    for q in ("sp", "pool")}
        S = Sched(nc, eng_sems, dma_sems)
        psM = [top.enter_context(nc.psum_tensor(f"psM{i}", [128, 512], F32)) for i in range(6)]
        psT = [top.enter_context(nc.psum_tensor(f"psT{i}", [128, 1024], BF)) for i in range(2)]
        cb = sb(top, "cb_sb", [128, CB_W], BF)
        cf = sb(top, "cf_sb", [128, CF_W], F32)
        junk = sb(top, "junk", [128, D], F32)
        ident = cb[:, CB_ID:CB_ID + 128]
        rotA = cb[:, CB_ROTA:CB_ROTA + 128]
        rotR = cb[:, CB_ROTR:CB_ROTR + 128]
        maskb = cb[:, CB_MASK:CB_MASK + 256]
        onesz = [cb[:, CB_ONE0:CB_ONE0 + 128], cb[:, CB_ONE1:CB_ONE1 + 128]]

        def mm(ps, lhsT, rhs, start, stop):
            return lambda e: e.matmul(ps, lhsT=lhsT, rhs=rhs, start=start, stop=stop)

        def rmsnorm_chunk(xin_ap, xin_res, g_ap, h_out, h_res, ss, sd, rs, col, tag):
            S.add("act", lambda e: e.activation(out=junk[:], in_=xin_ap, func=AF.Square,
                                                accum_out=ss[:, col:col + 1]),
                  reads=[xin_res, "junk"], writes=[("ss", tag, col), "junk"])
            S.add("act", lambda e: e.activation(out=sd[:, col:col + 1], in_=ss[:, col:col + 1],
                                                func=AF.Sqrt, bias=epsb, scale=1.0 / D),
                  reads=[("ss", tag, col), "eps"], writes=[("sd", tag, col)])
            S.add("dve", lambda e: e.reciprocal(out=rs[:, col:col + 1], in_=sd[:, col:col + 1]),
                  reads=[("sd", tag, col)], writes=[("rs", tag, col)])
            S.add("dve", lambda e: e.scalar_tensor_tensor(out=h_out, in0=xin_ap, scalar=rs[:, col:col + 1],
                                                          in1=g_ap, op0=ALU.mult, op1=ALU.mult),
                  reads=[xin_res, ("rs", tag, col), "gvec"], writes=[h_res])

        epst = sb(top, "epst", [128, 1], F32)
        epsb = epst[:, 0:1]

        hstack = contextlib.ExitStack()
        hT = sb(hstack, "hT", [128, 8, S_LEN], BF)
        Wc0 = sb(hstack, "Wc0", [128, 8, 512], BF)
        w_in_v = w_in.rearrange("(k p) c -> p k c", p=128)
        gtypes = ["aq"] * 3 + ["ak"] * 3 + ["av"] * 3 + ["rq", "rk", "rv", "rv"] + ["gr"] * 2 + ["ga"] * 2 + ["gg"] * 2
        for ph in _phase("A", phases):
            xin = [sb(ph, f"xin{i}", [128, D], F32) for i in range(4)]
            hb = [sb(ph, f"hb{i}", [128, D], BF) for i in range(4)]
            g1 = sb(ph, "g1", [128, 8], F32)
            ss = sb(ph, "ssA", [128, NCH], F32)
            sd = sb(ph, "sdA", [128, NCH], F32)
            rs = sb(ph, "rsA", [128, NCH], F32)
            cosT = sb(ph, "cosT", [128, S_LEN], F32)
            sinT = sb(ph, "sinT", [128, S_LEN], F32)
            Wb = [sb(ph, f"Wb{i}", [128, 8, 512], BF) for i in range(4)]
            Ub = [sb(ph, f"Ub{i}", [128, 512], BF) for i in range(2)]
            Vb_ = [sb(ph, f"Vb{i}", [128, 512], BF) for i in range(2)]
            stF = [sb(ph, f"stF{i}", [128, S_LEN], BF) for i in range(2)]
            stT = [sb(ph, f"stT{i}", [128, 512], BF) for i in range(3)]
            ktm = sb(ph, "ktm", [128, NCH, 128], BF)

            S.add("dve", lambda e: e.memset(epst[:], EPS), writes=["eps"])
            S.add("pool", lambda e: e.dma_start(out=cb[:], in_=cbd[:, :]), writes=["cb"], dma=True)
            S.add("sp", lambda e: e.dma_start(out=cf[:], in_=cfd[:, :]), writes=["cf"], dma=True)
            S.add("sp", lambda e: e.dma_start(out=g1[:], in_=g1Td[:, :]), writes=["gvec"], dma=True)

            w_in_v = w_in.rearrange("(k p) c -> p k c", p=128)

            order = [6, 7, 8, 0, 1, 2, 3, 4, 5, 11, 12, 9, 10, 13, 14]

            def load_w(pos):
                wg = order[pos]
                buf = Wb[pos % 4]
                for k0 in range(0, 8, 4):
                    S.add("pool", lambda e, buf=buf, k0=k0, wg=wg: e.dma_start(
                        out=buf[:, k0:k0 + 4, :], in_=w_in_v[:, k0:k0 + 4, wg * 512:(wg + 1) * 512]),
                        writes=[("W", pos % 4)], dma=True)

            for pos_ in range(4):
                load_w(pos_)

            pm = [0]
            stc = [0]

            def tokmajor_chunk(pos, c, hres):
                wg = order[pos]
                W = Wb[pos % 4]
                wres = ("W", pos % 4)
                if gtypes[wg] == "av":
                    dst, c0 = vA, (wg - 6) * 512
                else:
                    dst, c0 = vR, (wg - 11) * 512
                ps = psM[pm[0] % 4]
                pres = ("psM", pm[0] % 4)
                pm[0] += 1
                for kc in range(8):
                    S.add("pe", mm(ps[:], hT[:, kc, c * 128:(c + 1) * 128], W[:, kc, :], kc == 0, kc == 7),
                          reads=[wres] + hres, writes=[pres])
                si = stc[0] % 3
                stc[0] += 1
                if stc[0] % 2 == 0 and not hres:
                    S.add("act", lambda e: e.activation(out=stT[si][:], in_=ps[:], func=AF.Copy),
                          reads=[pres], writes=[("stT", si)])
                else:
                    S.add("dve", lambda e: e.tensor_copy(out=stT[si][:], in_=ps[:]),
                          reads=[pres], writes=[("stT", si)])
                S.add("pool", lambda e: e.dma_start(out=dst[c * 128:(c + 1) * 128, c0:c0 + 512], in_=stT[si][:]),
                      reads=[("stT", si)], writes=["dscr"], dma=True)

            gtypes = ["aq"] * 3 + ["ak"] * 3 + ["av"] * 3 + ["rq", "rk", "rv", "rv"] + ["gr"] * 2 + ["ga"] * 2 + ["gg"] * 2

            def a1_front(c):
                b = c % 4
                S.add("sp", lambda e: e.dma_start(out=xin[b][:], in_=x[c * 128:(c + 1) * 128, :]),
                      writes=[("xin", b)], dma=True)
                S.add("act", lambda e: e.activation(out=junk[:], in_=xin[b][:], func=AF.Square,
                                                    accum_out=ss[:, c:c + 1]),
                      reads=[("xin", b), "junk"], writes=[("ss", c), "junk"])
                S.add("act", lambda e: e.activation(out=sd[:, c:c + 1], in_=ss[:, c:c + 1],
                                                    func=AF.Sqrt, bias=epsb, scale=1.0 / D),
                      reads=[("ss", c), "eps"], writes=[("sd", c)])
                S.add("dve", lambda e: e.reciprocal(out=rs[:, c:c + 1], in_=sd[:, c:c + 1]),
                      reads=[("sd", c)], writes=[("rs", c)])

            def a1_rest(c):
                b = c % 4
                pb = c % 2
                S.add("act", lambda e: e.activation(out=hb[b][:], in_=xin[b][:], func=AF.Copy,
                                                    scale=rs[:, c:c + 1]),
                      reads=[("xin", b), ("rs", c)], writes=[("hb", b)])
                for k in range(8):
                    S.add("pe", lambda e, k=k: e.transpose(psT[pb][:, k * 128:(k + 1) * 128],
                                                           hb[b][:, k * 128:(k + 1) * 128], ident),
                          reads=[("hb", b), "cb"], writes=[("psT", pb)])
                S.add("dve", lambda e: e.tensor_tensor(
                    out=hT[:, :, c * 128:(c + 1) * 128],
                    in0=psT[pb][:].rearrange("p (k t) -> p k t", k=8),
                    in1=g1[:, :].unsqueeze(2).broadcast_to([128, 8, 128]), op=ALU.mult),
                    reads=[("psT", pb), "gvec"], writes=[("hT", c)])

            for c in range(NCH + 2):
                if c == 6:
                    S.add("sp", lambda e: e.dma_start(out=cosT[:], in_=cosAd[:, :]), writes=["tab"], dma=True)
                    S.add("sp", lambda e: e.dma_start(out=sinT[:], in_=sinAd[:, :]), writes=["tab"], dma=True)
                if c < NCH:
                    a1_front(c)
                if 1 <= c <= NCH:
                    a1_rest(c - 1)
                if c >= 2:
                    for pos_ in range(3):
                        tokmajor_chunk(pos_, c - 2, [("hT", c - 2)])

            p2 = [0]
            sf = [0]
            tpc = [0]
            pending = []
            for pos in range(3, len(order)):
                wg = order[pos]
                ty = gtypes[wg]
                if ty in ("av", "rv", "gr"):
                    for f_ in pending:
                        f_()
                    pending.clear()
                if 4 <= pos + 1 < len(order):
                    load_w(pos + 1)
                if pos == len(order) - 2:
                    for k0 in range(0, 8, 4):
                        S.add("pool", lambda e, k0=k0: e.dma_start(
                            out=Wc0[:, k0:k0 + 4, :], in_=w_in_v[:, k0:k0 + 4, 15 * 512:16 * 512]),
                            writes=["Wc0"], dma=True)
                if wg == 11:
                    S.add("sp", lambda e: e.dma_start(out=cosT[:], in_=cosRd[:, :]), writes=["tab"], dma=True)
                    S.add("sp", lambda e: e.dma_start(out=sinT[:], in_=sinRd[:, :]), writes=["tab"], dma=True)
                W = Wb[pos % 4]
                wres = ("W", pos % 4)
                if ty in ("av", "rv"):
                    for c in range(NCH):
                        tokmajor_chunk(pos, c, [])
                    continue
                for j in range(4):
                    stage = stF[sf[0] % 2]
                    sres = ("stF", sf[0] % 2)
                    sf[0] += 1
                    if ty == "aq":
                        dstc = qkA[wg * 4 + j]
                    elif ty == "ak":
                        dstc = qkA[12 + (wg - 3) * 4 + j]
                    elif ty == "rq":
                        dstc = qkR[j]
                    elif ty == "rk":
                        dstc = qkR[4 + j]
                    elif ty == "gr":
                        dstc = grT[(wg - 13) * 4 + j]
                    elif ty == "ga":
                        dstc = gaT[(wg - 15) * 4 + j]
                    else:
                        dstc = ggT[(wg - 17) * 4 + j]
                    for tt in range(NTT):
                        tsl = slice(tt * 512, (tt + 1) * 512)
                        ps = psM[pm[0] % 4]
                        pres = ("psM", pm[0] % 4)
                        pm[0] += 1
                        for kc in range(8):
                            S.add("pe", mm(ps[:], W[:, kc, j * 128:(j + 1) * 128], hT[:, kc, tsl], kc == 0, kc == 7),
                                  reads=[wres], writes=[pres])
                        if ty in ("aq", "ak", "rq", "rk"):
                            ub = p2[0] % 2
                            p2[0] += 1
                            rot = rotA if ty in ("aq", "ak") else rotR
                            S.add("dve", lambda e, ps=ps, ub=ub, tsl=tsl: e.tensor_tensor(
                                out=Ub[ub][:], in0=ps[:], in1=cosT[:, tsl], op=ALU.mult),
                                reads=[pres, "tab"], writes=[("U", ub)])
                            S.add("dve", lambda e, ps=ps, ub=ub, tsl=tsl: e.tensor_tensor(
                                out=Vb_[ub][:], in0=ps[:], in1=sinT[:, tsl], op=ALU.mult),
                                reads=[pres, "tab"], writes=[("V", ub)])
                            sc = RET_SCALE if ty == "rk" else 1.0
                            last = (tt == NTT - 1)

                            dil = GROUP_DIL[wg] if ty == "aq" else (GROUP_DIL[wg - 3] if ty == "ak" else 1)

                            def finish(ub=ub, rot=rot, stage=stage, sres=sres, tsl=tsl, sc=sc, ty=ty, tt=tt, j=j,
                                       last=last, dstc=dstc, dil=dil):
                                ps2 = psM[4 + ub]
                                S.add("pe", mm(ps2[:], ident, Ub[ub][:], True, False),
                                      reads=[("U", ub), "cb"], writes=[("psM", 4 + ub)])
                                S.add("pe", mm(ps2[:], rot, Vb_[ub][:], False, True),
                                      reads=[("V", ub), "cb"], writes=[("psM", 4 + ub)])
                                if dil > 1:
                                    w = 512 // dil
                                    o_ap = stage[:].rearrange("p (r l) -> p r l", r=dil)[:, :, tt * w:(tt + 1) * w]
                                    i_ap = ps2[:].rearrange("p (i r) -> p r i", r=dil)
                                else:
                                    o_ap = stage[:, tsl]
                                    i_ap = ps2[:]
                                S.add("act", lambda e: e.activation(out=o_ap, in_=i_ap, func=AF.Copy, scale=sc),
                                      reads=[("psM", 4 + ub)], writes=[sres])
                                if ty == "rk":
                                    tb = tpc[0] % 2
                                    tpc[0] += 1
                                    for c in range(4):
                                        S.add("pe", lambda e, c=c: e.transpose(
                                            psT[tb][:, c * 128:(c + 1) * 128],
                                            stage[:, tt * 512 + c * 128: tt * 512 + (c + 1) * 128], ident),
                                            reads=[sres, "cb"], writes=[("psT", tb)])
                                    S.add("dve", lambda e: e.tensor_copy(
                                        out=ktm[:, tt * 4:(tt + 1) * 4, :],
                                        in_=psT[tb][:, 0:512].rearrange("p (c d) -> p c d", c=4)),
                                        reads=[("psT", tb)], writes=["ktm"])
                                if last:
                                    S.add("pool", lambda e: e.dma_start(out=dstc[:, :], in_=stage[:]),
                                          reads=[sres], writes=["dscr"], dma=True)
                                    if ty == "rk":
                                        kv_ = kRt.rearrange("(n j) c -> j n c", j=128)
                                        for n0 in range(0, NCH, 8):
                                            S.add("pool", lambda e, n0=n0: e.dma_start(
                                                out=kv_[:, n0:n0 + 8, j * 128:(j + 1) * 128], in_=ktm[:, n0:n0 + 8, :]),
                                                reads=["ktm"], writes=["dscr"], dma=True)

                            for f_ in pending:
                                f_()
                            pending.clear()
                            pending.append(finish)
                        else:
                            fn = AF.Silu if ty == "gr" else AF.Sigmoid
                            S.add("act", lambda e, ps=ps, stage=stage, tsl=tsl, fn=fn: e.activation(
                                out=stage[:, tsl], in_=ps[:], func=fn),
                                reads=[pres], writes=[sres])
                            if tt == NTT - 1:
                                S.add("pool", lambda e, stage=stage, dstc=dstc: e.dma_start(out=dstc[:, :], in_=stage[:]),
                                      reads=[sres], writes=["dscr"], dma=True)
            for f_ in pending:
                f_()
            pending.clear()
            S.barrier()
            S.emit()

        for ph in _phase("C", phases):
            Wb2 = [Wc0] + [sb(ph, f"Wc{i}", [128, 8, 512], BF) for i in range(1, 3)]
            stF2 = [sb(ph, f"stC{i}", [128, S_LEN], BF) for i in range(2)]
            qT = sb(ph, "rqT", [128, S_LEN], BF)
            kT = sb(ph, "rkT", [128, S_LEN], BF)
            Kt = sb(ph, "rKt", [128, NCH, 128], BF)
            Vr = sb(ph, "rV", [128, NCH, 256], BF)
            gTb = [sb(ph, f"rgT{i}", [128, 2, 1024], BF) for i in range(2)]
            qxi = sb(ph, "rqxi", [128, NCH, 128], BF)
            Sf = [sb(ph, f"rSf{i}", [128, 256], F32) for i in range(2)]
            Sbr = [sb(ph, f"rSb{i}", [128, 256], BF) for i in range(6)]
            AD = [sb(ph, f"rAD{i}", [128, 128], BF) for i in range(4)]
            osb = sb(ph, "rosb", [128, NCH, 256], BF)
            s1 = sb(ph, "rs1", [128, NCH], F32)
            s2 = sb(ph, "rs2", [128, NCH], F32)
            mean = sb(ph, "rmean", [128, NCH], F32)
            msq = sb(ph, "rmsq", [128, NCH], F32)
            var = sb(ph, "rvar", [128, NCH], F32)
            vstd = sb(ph, "rvstd", [128, NCH], F32)
            vrs = sb(ph, "rvrs", [128, NCH], F32)
            rstb = [sb(ph, f"rst{i}", [128, 2, 1024], BF) for i in range(2)]
            gam = [float(np.exp(np.log1p(-(2.0 ** (-5.0 - h))))) for h in range(4)]
            kv_ = kRt.rearrange("(n j) c -> j n c", j=128)
            vv_ = vR.rearrange("(n j) c -> j n c", j=128)
            gorder = [15, 16, 17, 18]

            def load_wc(pos):
                wg = gorder[pos]
                for k0 in range(0, 8, 4):
                    S.add("pool", lambda e, k0=k0: e.dma_start(
                        out=Wb2[pos % 3][:, k0:k0 + 4, :], in_=w_in_v[:, k0:k0 + 4, wg * 512:(wg + 1) * 512]),
                        writes=[("Wc", pos % 3)], dma=True)

            def loadC(h):
                S.add("sp", lambda e: e.dma_start(out=qT[:], in_=qkR[h][:, :]), writes=["qT"], dma=True)
                S.add("sp", lambda e: e.dma_start(out=kT[:], in_=qkR[4 + h][:, :]), writes=["kT"], dma=True)
                for n0 in range(0, NCH, 8):
                    S.add("sp", lambda e, n0=n0: e.dma_start(
                        out=Kt[:, n0:n0 + 8, :], in_=kv_[:, n0:n0 + 8, h * 128:(h + 1) * 128]),
                        writes=["Kt"], dma=True)
                    S.add("sp", lambda e, n0=n0: e.dma_start(
                        out=Vr[:, n0:n0 + 8, :], in_=vv_[:, n0:n0 + 8, h * 256:(h + 1) * 256]),
                        writes=["Vr"], dma=True)

            SK = 3

            def load_g(h, blk):
                for a in range(2):
                    S.add("sp", lambda e, a=a: e.dma_start(
                        out=gTb[blk % 2][:, a, :], in_=grT[2 * h + a][:, blk * 1024:(blk + 1) * 1024]),
                        writes=[("gT", blk % 2)], dma=True)

            def retention_steps():
                gi = [0]
                for h in range(4):
                    cd = gam[h] ** 128
                    if h == 0:
                        loadC(0)
                        S.add("dve", lambda e: e.tensor_scalar(
                            out=Kt[:], in0=Kt[:], scalar1=cf[:, CF_ZETA:CF_ZETA + 1], scalar2=None, op0=ALU.mult),
                            reads=["Kt", "cf"], writes=["Kt"])
                        S.add("dve", lambda e: e.tensor_tensor(
                            out=qxi[:], in0=qT[:].rearrange("p (n i) -> p n i", n=NCH),
                            in1=cf[:, CF_XI:CF_XI + 128].unsqueeze(1).broadcast_to([128, NCH, 128]),
                            op=ALU.mult), reads=["qT", "cf"], writes=["qxi"])
                    yield
                    for n in range(NCH + SK):
                        if n == 4:
                            load_g(h, 0)
                        if n < NCH:
                            csl = slice(n * 128, (n + 1) * 128)
                            bank = psM[2 + n % 2]
                            bres = ("psM", 2 + n % 2)
                            kvp = bank[:, 0:256]
                            atp = bank[:, 256:384]
                            S.add("pe", mm(kvp, Kt[:, n, :], Vr[:, n, :], True, True),
                                  reads=["Kt", "Vr"], writes=[bres])
                            S.add("pe", mm(atp, kT[:, csl], qT[:, csl], True, True),
                                  reads=["kT", "qT"], writes=[bres])
                            if n == 0:
                                S.add("dve", lambda e, kvp=kvp: e.tensor_copy(out=Sf[0][:], in_=kvp),
                                      reads=[bres], writes=[("Sf", 0)])
                            else:
                                S.add("dve", lambda e, kvp=kvp, n=n, cd=cd: e.scalar_tensor_tensor(
                                    out=Sf[n % 2][:], in0=Sf[(n - 1) % 2][:], scalar=cd, in1=kvp,
                                    op0=ALU.mult, op1=ALU.add),
                                    reads=[bres, ("Sf", (n - 1) % 2)], writes=[("Sf", n % 2)])
                            if n < NCH - 1:
                                S.add("dve", lambda e, n=n: e.tensor_copy(out=Sbr[(n + 1) % 6][:], in_=Sf[n % 2][:]),
                                      reads=[("Sf", n % 2)], writes=[("Sb", (n + 1) % 6)])
                            S.add("dve", lambda e, atp=atp, n=n, h=h: e.tensor_tensor(
                                out=AD[n % 4][:], in0=atp, in1=cf[:, CF_DEC + h * 128:CF_DEC + (h + 1) * 128],
                                op=ALU.mult), reads=[bres, "cf"], writes=[("AD", n % 4)])
                        if n >= SK:
                            m = n - SK
                            op_ = psM[4 + m % 2]
                            S.add("pe", mm(op_[:, 0:256], AD[m % 4][:], Vr[:, m, :], True, m == 0),
                                  reads=[("AD", m % 4), "Vr"], writes=[("psM", 4 + m % 2)])
                            if m > 0:
                                S.add("pe", mm(op_[:, 0:256], qxi[:, m, :], Sbr[m % 6][:], False, True),
                                      reads=["qxi", ("Sb", m % 6)], writes=[("psM", 4 + m % 2)])
                            S.add("act", lambda e, op_=op_, m=m: e.activation(out=osb[:, m, :], in_=op_[:, 0:256],
                                                                              func=AF.Copy, accum_out=s1[:, m:m + 1]),
                                  reads=[("psM", 4 + m % 2)], writes=["osb", "s1"])
                            S.add("act", lambda e, op_=op_, m=m: e.activation(out=junk[:, 0:256], in_=op_[:, 0:256],
                                                                              func=AF.Square, accum_out=s2[:, m:m + 1]),
                                  reads=[("psM", 4 + m % 2), "junk"], writes=["s2", "junk"])
                        yield
                    load_g(h, 1)
                    if h + 1 < 4:
                        loadC(h + 1)
                    S.add("dve", lambda e: e.tensor_scalar(out=mean[:], in0=s1[:], scalar1=1.0 / 256, scalar2=None,
                                                           op0=ALU.mult), reads=["s1"], writes=["mean"])
                    S.add("dve", lambda e: e.tensor_tensor(out=msq[:], in0=mean[:], in1=mean[:], op=ALU.mult),
                          reads=["mean"], writes=["msq"])
                    S.add("dve", lambda e: e.scalar_tensor_tensor(out=var[:], in0=s2[:], scalar=1.0 / 256, in1=msq[:],
                                                                  op0=ALU.mult, op1=ALU.subtract),
                          reads=["s2", "msq"], writes=["var"])
                    S.add("act", lambda e: e.activation(out=vstd[:], in_=var[:], func=AF.Sqrt, bias=epsb, scale=1.0),
                          reads=["var"], writes=["vstd"])
                    S.add("dve", lambda e: e.reciprocal(out=vrs[:], in_=vstd[:]), reads=["vstd"], writes=["vrs"])
                    S.add("dve", lambda e: e.scalar_tensor_tensor(out=msq[:], in0=mean[:], scalar=-1.0, in1=vrs[:],
                                                                  op0=ALU.mult, op1=ALU.mult),
                          reads=["mean", "vrs"], writes=["nb"])
                    for _ in range(10):
                        yield

                    def normalise(n):
                        S.add("act", lambda e: e.activation(
                            out=osb[:, n, :], in_=osb[:, n, :], func=AF.Identity,
                            scale=vrs[:, n:n + 1], bias=msq[:, n:n + 1]),
                            reads=["osb", "nb", "vrs"], writes=[("osbn", n)])
                    normalise(0)
                    normalise(1)
                    yield
                    yield
                    for blk in range(4):
                        gb = blk % 2
                        bsl = slice(blk * 1024, (blk + 1) * 1024)
                        if 1 <= blk and blk + 1 < 4:
                            load_g(h, blk + 1)
                        for nn in range(8):
                            n = blk * 8 + nn
                            if n + 2 < NCH:
                                normalise(n + 2)
                            tb = n % 2
                            for a in range(2):
                                S.add("pe", lambda e, n=n, a=a, tb=tb: e.transpose(
                                    psT[tb][:, a * 128:(a + 1) * 128], osb[:, n, a * 128:(a + 1) * 128], ident),
                                    reads=[("osbn", n), "cb"], writes=[("psT", tb)])
                            S.add("dve", lambda e, nn=nn, tb=tb, gb=gb: e.tensor_tensor(
                                out=rstb[gb][:, :, nn * 128:(nn + 1) * 128],
                                in0=psT[tb][:, 0:256].rearrange("p (a t) -> p a t", a=2),
                                in1=gTb[gb][:, :, nn * 128:(nn + 1) * 128], op=ALU.mult),
                                reads=[("psT", tb), ("gT", gb)], writes=[("rst", gb)])
                            if n == 12 and h + 1 < 4:
                                S.add("dve", lambda e, h=h: e.tensor_scalar(
                                    out=Kt[:], in0=Kt[:], scalar1=cf[:, CF_ZETA + h + 1:CF_ZETA + h + 2], scalar2=None,
                                    op0=ALU.mult), reads=["Kt", "cf"], writes=["Kt"])
                                S.add("dve", lambda e, h=h: e.tensor_tensor(
                                    out=qxi[:], in0=qT[:].rearrange("p (n i) -> p n i", n=NCH),
                                    in1=cf[:, CF_XI + (h + 1) * 128:CF_XI + (h + 2) * 128].unsqueeze(1).broadcast_to(
                                        [128, NCH, 128]),
                                    op=ALU.mult), reads=["qT", "cf"], writes=["qxi"])
                            yield
                        for a in range(2):
                            S.add("pool", lambda e, h=h, a=a, gb=gb, bsl=bsl: e.dma_start(
                                out=retT[2 * h + a][:, bsl], in_=rstb[gb][:, a, :]),
                                reads=[("rst", gb)], writes=["dscr"], dma=True)

            ret = retention_steps()

            def adv(k):
                for _ in range(k):
                    try:
                        next(ret)
                    except StopIteration:
                        return

            load_wc(1)
            adv(1)
            pmc = [0]
            sfc = [0]
            git = [0]
            for pos in range(4):
                wg = gorder[pos]
                if pos + 2 < 4:
                    load_wc(pos + 2)
                W = Wb2[pos % 3]
                wres = ("Wc", pos % 3)
                for j in range(4):
                    stage = stF2[sfc[0] % 2]
                    sres = ("stC", sfc[0] % 2)
                    sfc[0] += 1
                    dstc = gaT[(wg - 15) * 4 + j] if wg < 17 else ggT[(wg - 17) * 4 + j]
                    for tt in range(NTT):
                        tsl = slice(tt * 512, (tt + 1) * 512)
                        ps = psM[pmc[0] % 2]
                        pres = ("psM", pmc[0] % 2)
                        pmc[0] += 1
                        for kc in range(8):
                            S.add("pe", mm(ps[:], W[:, kc, j * 128:(j + 1) * 128], hT[:, kc, tsl], kc == 0, kc == 7),
                                  reads=[wres], writes=[pres])
                            if kc == 3 and git[0] >= 10:
                                adv(1)
                        S.add("act", lambda e, ps=ps, stage=stage, tsl=tsl: e.activation(
                            out=stage[:, tsl], in_=ps[:], func=AF.Sigmoid), reads=[pres], writes=[sres])
                        if git[0] >= 64:
                            adv(2)
                        elif git[0] >= 10:
                            adv(2 if git[0] % 2 == 0 else 1)
                        git[0] += 1
                    S.add("pool", lambda e, stage=stage, dstc=dstc: e.dma_start(out=dstc[:, :], in_=stage[:]),
                          reads=[sres], writes=["dscr"], dma=True)
            adv(10 ** 6)
            S.barrier()
            S.emit()
        hstack.close()

        bd = contextlib.ExitStack()
        Woa = sb(bd, "Woa", [128, 4, D], BF)
        Wor = sb(bd, "Wor", [128, 8, D], BF)
        Wo = sb(bd, "Wo", [128, 8, D], BF)
        for ph in _phase("B", phases):
            qTb = [sb(ph, f"qTb{i}", [128, S_LEN], BF) for i in range(2)]
            kTz = [[sb(ph, f"kTz{i}{e}", [128, S_LEN], BF) for e in range(2)] for i in range(2)]
            Vz = [[sb(ph, f"Vz{i}{e}", [128, NCH, 128], BF) for e in range(2)] for i in range(2)]
            NUMLs = [sb(ph, f"NUML{i}", [128, 2, S_LEN], F32) for i in range(2)]
            Pb = [sb(ph, f"Pb{i}", [128, 512], BF) for i in range(6)]
            rec = [sb(ph, f"rec{i}", [128, 512], F32) for i in range(1)]
            lnb = [sb(ph, f"lnb{i}", [128, 512], F32) for i in range(1)]
            oTs = [sb(ph, f"oTs{i}", [128, 512], BF) for i in range(2)]

            for i in range(2):
                S.add("dve", lambda e, i=i: e.memset(kTz[i][0][64:128, :], 0.0), writes=[("kTz", i)])
                S.add("dve", lambda e, i=i: e.memset(kTz[i][1][0:64, :], 0.0), writes=[("kTz", i)])
                S.add("pool", lambda e, i=i: e.memset(Vz[i][0][:, :, 64:128], 1.0), writes=[("Vz", i)])
                S.add("pool", lambda e, i=i: e.memset(Vz[i][1][:, :, 0:64], 1.0), writes=[("Vz", i)])

            units = [(hp, g) for hp in range(4) for g in range(3)]

            def load_unit(ui):
                hp, g = units[ui]
                b = ui % 2
                d = GROUP_DIL[g]
                nbk = NCH // d
                uq = g * 4 + hp
                uk = 12 + g * 4 + hp
                S.add("sp", lambda e: e.dma_start(out=qTb[b][:], in_=qkA[uq][:, :]), writes=[("qT", b)], dma=True)
                S.add("sp", lambda e: e.dma_start(out=kTz[b][0][0:64, :], in_=qkA[uk][0:64, :]),
                      writes=[("kTz", b)], dma=True)
                S.add("sp", lambda e: e.dma_start(out=kTz[b][1][64:128, :], in_=qkA[uk][64:128, :]),
                      writes=[("kTz", b)], dma=True)
                vv = vA.rearrange("(n j r) c -> j r n c", j=128, r=d)
                for e_ in range(2):
                    c0 = g * 512 + hp * 128 + e_ * 64
                    for kb0 in range(0, NCH, 8):
                        dst = Vz[b][e_][:, kb0:kb0 + 8, e_ * 64:(e_ + 1) * 64]
                        if nbk >= 8:
                            r, n0 = divmod(kb0, nbk)
                            src = vv[:, r, n0:n0 + 8, c0:c0 + 64]
                        else:
                            r0 = kb0 // nbk
                            nr = 8 // nbk
                            for rr in range(nr):
                                src = vv[:, r0 + rr, :, c0:c0 + 64]
                                dst = Vz[b][e_][:, kb0 + rr * nbk:kb0 + (rr + 1) * nbk, e_ * 64:(e_ + 1) * 64]
                                S.add("sp", lambda e, dst=dst, src=src: e.dma_start(out=dst, in_=src),
                                      writes=[("Vz", b)], dma=True)
                            continue
                        S.add("sp", lambda e, dst=dst, src=src: e.dma_start(out=dst, in_=src),
                              writes=[("Vz", b)], dma=True)

            steps = [(ui, kb) for ui in range(len(units)) for kb in range(NCH)]

            def pos(d, kb):
                nbk = NCH // d
                r, n = divmod(kb, nbk)
                return r, n, nbk, d * 128 * n + r

            def emit_S(si):
                ui, kb = steps[si]
                hp, g = units[ui]
                b = ui % 2
                d = GROUP_DIL[g]
                r, n, nbk, st = pos(d, kb)
                nq = 256 if n < nbk - 1 else 128
                ps = psM[si % 3]
                c0 = r * (S_LEN // d) + 128 * n
                kcols = slice(c0, c0 + 128)
                qcols = slice(c0, c0 + nq)
                for e_ in range(2):
                    S.add("pe", mm(ps[:, e_ * 256:e_ * 256 + nq], kTz[b][e_][:, kcols], qTb[b][:, qcols], True, False),
                          reads=[("kTz", b), ("qT", b)], writes=[("psM", si % 3)])
                    S.add("pe", mm(ps[:, e_ * 256:e_ * 256 + nq], ident, maskb[:, 0:nq], False, True),
                          reads=["cb"], writes=[("psM", si % 3)])
                P = Pb[si % 6]
                S.add("act", lambda e: e.activation(
                    out=P[:].rearrange("p (a q) -> p a q", a=2)[:, :, 0:nq],
                    in_=ps[:].rearrange("p (a q) -> p a q", a=2)[:, :, 0:nq],
                    func=AF.Exp, scale=0.125),
                    reads=[("psM", si % 3)], writes=[("P", si % 6)])

            def emit_PV(si):
                ui, kb = steps[si]
                hp, g = units[ui]
                b = ui % 2
                d = GROUP_DIL[g]
                r, n, nbk, st = pos(d, kb)
                pv = psM[3 + si % 2]
                pres = ("psM", 3 + si % 2)
                P = Pb[si % 6]
                Pp = Pb[(si - 1) % 6]
                NUML = NUMLs[hp % 2]
                for e_ in range(2):
                    terms = []
                    if n > 0:
                        terms.append((Vz[b][e_][:, kb - 1, :], Pp[:, e_ * 256 + 128:e_ * 256 + 256], ("P", (si - 1) % 6)))
                    terms.append((Vz[b][e_][:, kb, :], P[:, e_ * 256:e_ * 256 + 128], ("P", si % 6)))
                    for ti, (lh, rh, rres) in enumerate(terms):
                        S.add("pe", mm(pv[:, e_ * 128:(e_ + 1) * 128], lh, rh, ti == 0, ti == len(terms) - 1),
                              reads=[rres, ("Vz", b)], writes=[pres])
                dst = NUML[:, :, st:st + d * 127 + 1:d]
                src = pv[:, 0:256].rearrange("p (a q) -> p a q", a=2)
                if g == 0:
                    S.add("dve", lambda e: e.tensor_copy(out=dst, in_=src), reads=[pres], writes=[("numl", hp)])
                else:
                    S.add("dve", lambda e: e.tensor_tensor(out=dst, in0=src, in1=dst, op=ALU.add),
                          reads=[pres], writes=[("numl", hp)])
                if g == 2 and kb == NCH - 1:
                    for q8 in range(8):
                        sl = slice(q8 * 512, (q8 + 1) * 512)

                        def item(sl=sl, hp=hp, NUML=NUML, q8=q8):
                            rb = 0
                            ob = q8 % 2
                            S.add("act", lambda e: e.activation(out=lnb[rb][0:64, :], in_=NUML[64:128, 0, sl], func=AF.Ln),
                                  reads=[("numl", hp)], writes=[("lnb", rb)])
                            S.add("act", lambda e: e.activation(out=lnb[rb][64:128, :], in_=NUML[0:64, 1, sl], func=AF.Ln),
                                  reads=[("numl", hp)], writes=[("lnb", rb)])
                            S.add("act", lambda e: e.activation(out=rec[rb][:], in_=lnb[rb][:], func=AF.Exp, scale=-1.0),
                                  reads=[("lnb", rb)], writes=[("rec", rb)])
                            S.add("dve", lambda e: e.tensor_tensor(out=oTs[ob][0:64, :], in0=NUML[0:64, 0, sl],
                                                                   in1=rec[rb][0:64, :], op=ALU.mult),
                                  reads=[("rec", rb), ("numl", hp)], writes=[("oTs", ob)])
                            S.add("dve", lambda e: e.tensor_tensor(out=oTs[ob][64:128, :], in0=NUML[64:128, 1, sl],
                                                                   in1=rec[rb][64:128, :], op=ALU.mult),
                                  reads=[("rec", rb), ("numl", hp)], writes=[("oTs", ob)])
                            S.add("sp", lambda e: e.dma_start(out=oTd[hp][:, sl], in_=oTs[ob][:]),
                                  reads=[("oTs", ob)], writes=["dscr"], dma=True)
                        deferred.append(item)

            deferred = []
            load_unit(0)
            S.add("pool", lambda e: e.dma_start(out=Woa[:], in_=w_oa.rearrange("(k p) c -> p k c", p=128)),
                  writes=["Woa"], dma=True)
            for k0 in range(0, 8, 4):
                S.add("pool", lambda e, k0=k0: e.dma_start(
                    out=Wor[:, k0:k0 + 4, :], in_=w_or.rearrange("(k p) c -> p k c", p=128)[:, k0:k0 + 4, :]),
                    writes=["Wor"], dma=True)
            for k0 in range(0, 8, 4):
                S.add("pool", lambda e, k0=k0: e.dma_start(
                    out=Wo[:, k0:k0 + 4, :], in_=w_o.rearrange("(k p) c -> p k c", p=128)[:, k0:k0 + 4, :]),
                    writes=["Wo"], dma=True)
            NS = len(steps)
            for si in range(NS + 2):
                if si < NS:
                    emit_S(si)
                if si >= 2:
                    emit_PV(si - 2)
                    if deferred and si % 3 == 0:
                        deferred.pop(0)()
                if si % NCH == 1 and si // NCH + 1 < len(units):
                    load_unit(si // NCH + 1)
            while deferred:
                deferred.pop(0)()
            S.barrier()
            S.emit()

        for ph in _phase("D", phases):
            g2 = sb(ph, "g2", [128, D], F32)
            rtT = [sb(ph, f"rtT{i}", [128, 8, 512], BF) for i in range(2)]
            sgA = [sb(ph, f"sgA{i}", [128, 8, 512], BF) for i in range(2)]
            sgR = [sb(ph, f"sgR{i}", [128, 8, 512], BF) for i in range(2)]
            oTt = [sb(ph, f"oTt{i}", [128, 4, 512], BF) for i in range(2)]
            t1 = [sb(ph, f"t1{i}", [128, 512], F32) for i in range(2)]
            t2 = [sb(ph, f"t2{i}", [128, 512], F32) for i in range(2)]
            mgs = [sb(ph, f"mg{i}", [128, 8, 512], BF) for i in range(2)]
            xc = [sb(ph, f"xc{i}", [128, D], F32) for i in range(3)]
            x1c = [sb(ph, f"x1c{i}", [128, D], F32) for i in range(3)]
            h2b = [sb(ph, f"h2b{i}", [128, D], BF) for i in range(4)]
            h2s = [sb(ph, f"h2s{i}", [128, 8, 512], BF) for i in range(2)]
            ssD = sb(ph, "ssD", [128, NCH], F32)
            sdD = sb(ph, "sdD", [128, NCH], F32)
            rsD = sb(ph, "rsD", [128, NCH], F32)
            S.add("sp", lambda e: e.dma_start(out=g2[:], in_=g2d[:, :]), writes=["gvec"], dma=True)

            def loadD(tt):
                b = tt % 2
                tsl = slice(tt * 512, (tt + 1) * 512)
                S.add("sp", lambda e: e.dma_start(out=oTt[b][:], in_=oTd.rearrange("k p t -> p k t")[:, :, tsl]),
                      writes=[("oTt", b)], dma=True)
                S.add("sp", lambda e: e.dma_start(out=rtT[b][:], in_=retT.rearrange("k p t -> p k t")[:, :, tsl]),
                      writes=[("rtT", b)], dma=True)
                S.add("sp", lambda e: e.dma_start(out=sgA[b][:], in_=gaT.rearrange("k p t -> p k t")[:, :, tsl]),
                      writes=[("sgA", b)], dma=True)
                S.add("sp", lambda e: e.dma_start(out=sgR[b][:], in_=ggT.rearrange("k p t -> p k t")[:, :, tsl]),
                      writes=[("sgR", b)], dma=True)

            loadD(0)
            xi_ = [0]
            deferred = []
            for tt in range(NTT):
                b = tt % 2
                mg = mgs[b]
                mres = ("mg", b)
                tsl = slice(tt * 512, (tt + 1) * 512)
                for fo in range(8):
                    if fo == 4 and tt + 1 < NTT:
                        loadD(tt + 1)
                    fsl = slice(fo * 128, (fo + 1) * 128)
                    pa = psM[fo % 2]
                    pr = psM[2 + fo % 2]
                    for kc in range(4):
                        S.add("pe", mm(pa[:], Woa[:, kc, fsl], oTt[b][:, kc, :], kc == 0, kc == 3),
                              reads=["Woa", ("oTt", b)], writes=[("psM", fo % 2)])
                    for kc in range(8):
                        S.add("pe", mm(pr[:], Wor[:, kc, fsl], rtT[b][:, kc, :], kc == 0, kc == 7),
                              reads=["Wor", ("rtT", b)], writes=[("psM", 2 + fo % 2)])
                    tb = fo % 2
                    S.add("dve", lambda e, pa=pa, tb=tb, fo=fo, b=b: e.tensor_tensor(
                        out=t1[tb][:], in0=pa[:], in1=sgA[b][:, fo, :], op=ALU.mult),
                        reads=[("psM", fo % 2), ("sgA", b)], writes=[("t1", tb)])
                    S.add("dve", lambda e, pr=pr, tb=tb, fo=fo, b=b: e.tensor_tensor(
                        out=t2[tb][:], in0=pr[:], in1=sgR[b][:, fo, :], op=ALU.mult),
                        reads=[("psM", 2 + fo % 2), ("sgR", b)], writes=[("t2", tb)])
                    S.add("pool", lambda e, tb=tb, fo=fo, mg=mg: e.tensor_tensor(
                        out=mg[:, fo, :], in0=t1[tb][:], in1=t2[tb][:], op=ALU.add),
                        reads=[("t1", tb), ("t2", tb)], writes=[mres])
                hs = h2s[tt % 2]
                for c in range(4):
                    cg = tt * 4 + c
                    xb = xi_[0] % 3
                    xi_[0] += 1
                    S.add("sp", lambda e, xb=xb, cg=cg: e.dma_start(out=xc[xb][:], in_=x[cg * 128:(cg + 1) * 128, :]),
                          writes=[("xc", xb)], dma=True)
                    for half in range(2):
                        pd = psM[4 + half]
                        for kc in range(8):
                            S.add("pe", mm(pd[:], mg[:, kc, c * 128:(c + 1) * 128], Wo[:, kc, half * 512:(half + 1) * 512],
                                           kc == 0, kc == 7), reads=[mres, "Wo"], writes=[("psM", 4 + half)])
                        S.add("dve", lambda e, pd=pd, xb=xb, half=half: e.tensor_tensor(
                            out=x1c[xb][:, half * 512:(half + 1) * 512], in0=pd[:],
                            in1=xc[xb][:, half * 512:(half + 1) * 512], op=ALU.add),
                            reads=[("psM", 4 + half), ("xc", xb)], writes=[("x1c", xb)])
                    S.add("pool", lambda e, xb=xb, cg=cg: e.dma_start(out=x1d[cg * 128:(cg + 1) * 128, :], in_=x1c[xb][:]),
                          reads=[("x1c", xb)], writes=["dscr"], dma=True)
                    hb_ = cg % 4
                    rmsnorm_chunk(x1c[xb][:], ("x1c", xb), g2[:], h2b[hb_][:], ("h2b", hb_), ssD, sdD, rsD, cg, "D")

                    def item(hb_=hb_, c=c, hs=hs, tt=tt, tsl=tsl, cg=cg):
                        pt = cg % 2
                        for k in range(8):
                            S.add("pe", lambda e, k=k: e.transpose(
                                psT[pt][:, k * 128:(k + 1) * 128], h2b[hb_][:, k * 128:(k + 1) * 128], ident),
                                reads=[("h2b", hb_), "cb"], writes=[("psT", pt)])
                        S.add("act", lambda e: e.activation(
                            out=hs[:, :, c * 128:(c + 1) * 128],
                            in_=psT[pt][:].rearrange("p (k t) -> p k t", k=8), func=AF.Copy),
                            reads=[("psT", pt)], writes=[("h2s", tt % 2)])
                        if c == 3:
                            S.add("pool", lambda e: e.dma_start(
                                out=h2T.rearrange("k p t -> p k t")[:, :, tsl], in_=hs[:]),
                                reads=[("h2s", tt % 2)], writes=["dscr"], dma=True)
                    deferred.append(item)
                    while len(deferred) > 2:
                        deferred.pop(0)()
            while deferred:
                deferred.pop(0)()
            S.barrier()
            S.emit()

        bd.close()

        for ph in _phase("E", phases):
            Wg = sb(ph, "Wg", [128, 8, FFN], BF)
            Wu = sb(ph, "Wu", [128, 8, FFN], BF)
            Wd = sb(ph, "Wd", [128, NJ, D], BF)
            gF = sb(ph, "gF", [128, D], F32)
            h2t = [sb(ph, f"h2t{i}", [128, 8, 512], BF) for i in range(2)]
            act = sb(ph, "actb", [128, NJ, 512], BF)
            sg = [sb(ph, f"sg{i}", [128, 512], BF) for i in range(2)]
            x1t = [sb(ph, f"x1t{i}", [128, D], F32) for i in range(2)]
            x2 = x1t
            yo = x1t
            ssE = sb(ph, "ssE", [128, NCH], F32)
            sdE = sb(ph, "sdE", [128, NCH], F32)
            rsE = sb(ph, "rsE", [128, NCH], F32)
            wgv = w_g.rearrange("(k p) c -> p k c", p=128)
            wuv = w_u.rearrange("(k p) c -> p k c", p=128)
            wdv = w_d.rearrange("(j p) c -> p j c", p=128)
            for jb in range(NJ // 2):
                csl = slice(jb * 256, (jb + 1) * 256)
                S.add("pool", lambda e, csl=csl: e.dma_start(out=Wg[:, :, csl], in_=wgv[:, :, csl]),
                      writes=[("Wg", jb)], dma=True)
                S.add("pool", lambda e, csl=csl: e.dma_start(out=Wu[:, :, csl], in_=wuv[:, :, csl]),
                      writes=[("Wu", jb)], dma=True)
            for j0 in range(0, NJ, 2):
                S.add("pool", lambda e, j0=j0: e.dma_start(out=Wd[:, j0:j0 + 2, :], in_=wdv[:, j0:j0 + 2, :]),
                      writes=[("Wd", j0 // 2)], dma=True)
            S.add("sp", lambda e: e.dma_start(out=gF[:], in_=gFd[:, :]), writes=["gvec"], dma=True)

            def loadE(tt):
                b = tt % 2
                tsl = slice(tt * 512, (tt + 1) * 512)
                S.add("sp", lambda e: e.dma_start(out=h2t[b][:], in_=h2T.rearrange("k p t -> p k t")[:, :, tsl]),
                      writes=[("h2t", b)], dma=True)

            loadE(0)
            for tt in range(NTT):
                b = tt % 2
                if tt + 1 < NTT:
                    loadE(tt + 1)
                for j in range(NJ):
                    jsl = slice(j * 128, (j + 1) * 128)
                    pg = psM[j % 2]
                    pu = psM[2 + j % 2]
                    for kc in range(8):
                        S.add("pe", mm(pg[:], Wg[:, kc, jsl], h2t[b][:, kc, :], kc == 0, kc == 7),
                              reads=[("Wg", j // 2), ("h2t", b)], writes=[("psM", j % 2)])
                    for kc in range(8):
                        S.add("pe", mm(pu[:], Wu[:, kc, jsl], h2t[b][:, kc, :], kc == 0, kc == 7),
                              reads=[("Wu", j // 2), ("h2t", b)], writes=[("psM", 2 + j % 2)])
                    sb_ = j % 2
                    S.add("act", lambda e, pg=pg, sb_=sb_: e.activation(out=sg[sb_][:], in_=pg[:], func=AF.Silu),
                          reads=[("psM", j % 2)], writes=[("sg", sb_)])
                    S.add("dve", lambda e, pu=pu, sb_=sb_, j=j: e.tensor_tensor(
                        out=act[:, j, :], in0=pu[:], in1=sg[sb_][:], op=ALU.mult),
                        reads=[("psM", 2 + j % 2), ("sg", sb_)], writes=["act"])
                for c in range(4):
                    cg = tt * 4 + c
                    xb = cg % 2
                    S.add("sp", lambda e, xb=xb, cg=cg: e.dma_start(out=x1t[xb][:], in_=x1d[cg * 128:(cg + 1) * 128, :]),
                          writes=[("x1t", xb)], dma=True)
                    for half in range(2):
                        pd = psM[4 + half]
                        for j in range(NJ):
                            S.add("pe", mm(pd[:], act[:, j, c * 128:(c + 1) * 128], Wd[:, j, half * 512:(half + 1) * 512],
                                           j == 0, j == NJ - 1), reads=["act", ("Wd", j // 2)], writes=[("psM", 4 + half)])
                        S.add("dve", lambda e, pd=pd, xb=xb, half=half: e.tensor_tensor(
                            out=x2[xb][:, half * 512:(half + 1) * 512], in0=pd[:],
                            in1=x1t[xb][:, half * 512:(half + 1) * 512], op=ALU.add),
                            reads=[("psM", 4 + half), ("x1t", xb)], writes=[("x1t", xb)])
                    rmsnorm_chunk(x2[xb][:], ("x1t", xb), gF[:], yo[xb][:], ("x1t", xb), ssE, sdE, rsE, cg, "E")
                    S.add("sp", lambda e, xb=xb, cg=cg: e.dma_start(out=out[cg * 128:(cg + 1) * 128, :], in_=yo[xb][:]),
                          reads=[("x1t", xb)], writes=["outd"], dma=True)
            S.barrier()
            S.emit()
    return nc


def _consts():
    pos = np.arange(S_LEN, dtype=np.float64)
    inv = 10000.0 ** (-np.arange(0, 64, 2, dtype=np.float64) / 64.0)
    p = np.arange(128)
    angA = pos[None, :] * inv[(p % 64) % 32][:, None]
    cosA = np.cos(angA).astype(np.float32)
    sinA = np.sin(angA).astype(np.float32)
    base = 1.0 / (10000.0 ** np.linspace(0.0, 1.0, 64, dtype=np.float64))
    angR = pos[None, :] * base[p // 2][:, None]
    cosR = np.cos(angR).astype(np.float32)
    sinR = np.sin(angR).astype(np.float32)
    cb = np.zeros((128, CB_W), np.float32)
    cb[:, CB_ID:CB_ID + 128] = np.eye(128)
    for b in range(2):
        for m in range(64):
            if m < 32:
                cb[64 * b + m + 32, CB_ROTA + 64 * b + m] = -1.0
            else:
                cb[64 * b + m - 32, CB_ROTA + 64 * b + m] = 1.0
    for i in range(64):
        cb[2 * i + 1, CB_ROTR + 2 * i] = -1.0
        cb[2 * i, CB_ROTR + 2 * i + 1] = 1.0
    jj = np.arange(128)[:, None]
    qq = np.arange(128)[None, :]
    cb[:, CB_MASK:CB_MASK + 128] = np.where(jj <= qq, 0.0, NEG)
    cb[:, CB_MASK + 128:CB_MASK + 256] = np.where(jj >= qq, 0.0, NEG)
    cb[:, CB_ONE0:CB_ONE0 + 64] = 1.0
    cb[:, CB_ONE1 + 64:CB_ONE1 + 128] = 1.0
    cf = np.zeros((128, CF_W), np.float32)
    for h in range(4):
        lg = np.log1p(-(2.0 ** (-5.0 - h)))
        diff = (qq - jj).astype(np.float64)
        dec = np.where(diff >= 0, np.exp(np.maximum(diff, 0.0) * lg), 0.0)
        cf[:, CF_DEC + h * 128:CF_DEC + (h + 1) * 128] = dec
        cf[:, CF_ZETA + h] = np.exp((127 - np.arange(128)) * lg)
        cf[:, CF_XI + h * 128:CF_XI + (h + 1) * 128] = np.exp((np.arange(128) + 1.0) * lg)[None, :]
    return dict(cosA=cosA, sinA=sinA, cosR=cosR, sinR=sinR, cb=cb, cf=cf)


_CACHE = {}


def _run(inputs, debug=None):
    key = tuple(sorted(debug)) if debug else None
    if key not in _CACHE:
        _CACHE[key] = build_program(debug)
    nc = _CACHE[key]
    f = lambda a: np.ascontiguousarray(np.asarray(a, dtype=np.float32))
    x = f(inputs["x"])
    consts = _consts()
    shared = dict(
        w_in=f(inputs["w_in"])[0], w_out_attn=f(inputs["w_out_attn"])[0], w_out_ret=f(inputs["w_out_ret"])[0],
        w_out=f(inputs["w_out"])[0], w_ffn_gate=f(inputs["w_ffn_gate"])[0], w_ffn_up=f(inputs["w_ffn_up"])[0],
        w_ffn_down=f(inputs["w_ffn_down"])[0],
        g1T=np.ascontiguousarray(f(inputs["norm_mix_g"])[0].reshape(8, 128).T),
        g2rep=np.ascontiguousarray(np.broadcast_to(f(inputs["norm_ffn_g"])[0][None, :], (128, D))),
        gFrep=np.ascontiguousarray(np.broadcast_to(f(inputs["norm_final_g"])[None, :], (128, D))),
        **consts)
    in_maps = [dict(shared, x=np.ascontiguousarray(x[b])) for b in range(8)]
    res = run_bass_kernel_spmd(nc, in_maps, core_ids=list(range(8)))
    return res


def kernel(**inputs):
    res = _run(inputs)
    return np.stack([np.asarray(res.results[b]["out"], dtype=np.float32) for b in range(8)], axis=0)
```

```python
import numpy as np
import concourse.bass as bass
import concourse.mybir as mybir
from concourse.bass_utils import run_bass_kernel_spmd

F32 = mybir.dt.float32
BF = mybir.dt.bfloat16
AF = mybir.ActivationFunctionType
ALU = mybir.AluOpType

S_LEN = 4096
D = 1024
NCH = 32
NTT = 8
FFN = 2816
NJ = FFN // 128
EPS = 1e-6
GROUP_DIL = (1, 4, 16)
NEG = -30000.0
RET_SCALE = 128.0 ** -0.5
import os
CSTAGE = int(os.environ.get('CSTAGE', '9'))

CB_ID, CB_ROTA, CB_ROTR, CB_MASK, CB_ONE0, CB_ONE1, CB_W = 0, 128, 256, 384, 640, 768, 896
CF_DEC, CF_ZETA, CF_XI, CF_W = 0, 512, 516, 1028


def _phase(name, phases):
    import contextlib
    if name in phases:
        with contextlib.ExitStack() as st:
            yield st


class _Op:
    __slots__ = ("idx", "eng", "fn", "dma", "deps_eng", "deps_dma", "needs_inc",
                 "ticket", "sem", "semval", "slot_prev")


class Sched:
    NSLOT = 8

    def __init__(self, nc, eng_sems, dma_sems):
        self.nc = nc
        self.eng_sems = eng_sems
        self.dma_sems = dma_sems
        self.counter = {e: 0 for e in eng_sems}
        self.dma_count = {q: 0 for q in dma_sems}
        self.slot_cnt = {q: [0] * self.NSLOT for q in dma_sems}
        self.slot_last = {q: [None] * self.NSLOT for q in dma_sems}
        self.known = {e: {} for e in ("sp", "act", "dve", "pool", "pe")}
        self.begin()

    def begin(self):
        self.ops = []
        self.res = {}
        self.last = {}
        self.dmas = []

    def add(self, eng, fn, reads=(), writes=(), dma=False):
        o = _Op()
        o.idx = len(self.ops)
        o.eng = eng
        o.fn = fn
        o.dma = dma
        o.needs_inc = False
        o.ticket = None
        o.slot_prev = None
        de, dd = {}, set()

        def take(d_eng, d_dma):
            for e, i in d_eng.items():
                if de.get(e, -1) < i:
                    de[e] = i
            dd.update(d_dma)

        for r in reads:
            st = self.res.get(r)
            if st is not None:
                take(st[0], st[1])
        newep = []
        for w in writes:
            st = self.res.get(w)
            if st is not None and (st[2] or st[3]):
                take(st[2], st[3])
                newep.append(w)
        for w in writes:
            st = self.res.get(w)
            if st is None or w in newep:
                st = self.res[w] = [{}, [], {}, []]
            if dma:
                st[1].append(o.idx)
            else:
                st[0][eng] = o.idx
        for r in reads:
            if r in writes:
                continue
            st = self.res.get(r)
            if st is None:
                st = self.res[r] = [{}, [], {}, []]
            if dma:
                st[3].append(o.idx)
            else:
                st[2][eng] = o.idx
        if dma:
            q = eng
            k = self.dma_count[q] % self.NSLOT
            self.dma_count[q] += 1
            self.slot_cnt[q][k] += 1
            o.sem = self.dma_sems[q][k]
            o.semval = 16 * self.slot_cnt[q][k]
            o.slot_prev = self.slot_last[q][k]
            self.slot_last[q][k] = (o.sem, o.semval)
            self.dmas.append(o.idx)
        else:
            self.last[eng] = o.idx
        o.deps_eng = de
        o.deps_dma = dd
        for e, i in de.items():
            if not (eng == "pe" and e == "pe"):
                self.ops[i].needs_inc = True
        self.ops.append(o)
        return o

    def barrier(self):
        last = dict(self.last)
        dmas = list(self.dmas)
        for e in ("sp", "act", "dve", "pool", "pe"):
            o = self.add(e, None)
            o.deps_eng = dict(last)
            o.deps_dma = set(dmas)
            for i in last.values():
                self.ops[i].needs_inc = True

    def emit(self):
        nc = self.nc
        for o in self.ops:
            if not o.dma and o.needs_inc:
                self.counter[o.eng] += 1
                o.ticket = self.counter[o.eng]
        ops = self.ops

        def run(engname, e):
            known = self.known[engname]

            def need(sem, val):
                key = id(sem)
                if known.get(key, 0) >= val:
                    return
                e.wait_ge(sem, val)
                known[key] = val

            for o in ops:
                if o.eng != engname:
                    continue
                for en, i in o.deps_eng.items():
                    if engname == "pe" and en == "pe" and o.fn is not None:
                        continue
                    need(self.eng_sems[en], ops[i].ticket)
                for i in o.deps_dma:
                    need(ops[i].sem, ops[i].semval)
                if o.dma and o.slot_prev is not None:
                    need(*o.slot_prev)
                if o.fn is None:
                    continue
                inst = o.fn(e)
                if o.dma:
                    inst.then_inc(o.sem, 16)
                elif o.needs_inc:
                    inst.then_inc(self.eng_sems[engname], 1)

        with nc.Block() as block:
            @block.sync
            def _(e):
                run("sp", e)

            @block.scalar
            def _(e):
                run("act", e)

            @block.vector
            def _(e):
                run("dve", e)

            @block.gpsimd
            def _(e):
                run("pool", e)

            @block.tensor
            def _(e):
                run("pe", e)
        self.begin()


def build_program(debug=False, phases="ABCDE"):
    nc = bass.Bass("TRN2", target_bir_lowering=False)

    def din(name, shape, dt=F32):
        return nc.dram_tensor(name, list(shape), dt, kind="ExternalInput").ap()

    def dscr(name, shape, dt):
        kind = "ExternalOutput" if (debug and name in debug) else "Internal"
        return nc.dram_tensor(name, list(shape), dt, kind=kind).ap()

    x = din("x", [S_LEN, D])
    w_in = din("w_in", [D, 9728])
    w_oa = din("w_out_attn", [512, D])
    w_or = din("w_out_ret", [D, D])
    w_o = din("w_out", [D, D])
    w_g = din("w_ffn_gate", [D, FFN])
    w_u = din("w_ffn_up", [D, FFN])
    w_d = din("w_ffn_down", [FFN, D])
    g1Td = din("g1T", [128, 8])
    g2d = din("g2rep", [128, D])
    gFd = din("gFrep", [128, D])
    cosAd = din("cosA", [128, S_LEN])
    sinAd = din("sinA", [128, S_LEN])
    cosRd = din("cosR", [128, S_LEN])
    sinRd = din("sinR", [128, S_LEN])
    cbd = din("cb", [128, CB_W])
    cfd = din("cf", [128, CF_W])
    out = nc.dram_tensor("out", [S_LEN, D], F32, kind="ExternalOutput").ap()

    qkA = dscr("qkA", [24, 128, S_LEN], BF)
    vA = dscr("vA", [S_LEN, 1536], BF)
    qkR = dscr("qkR", [8, 128, S_LEN], BF)
    kRt = dscr("kRt", [S_LEN, 512], BF)
    vR = dscr("vR", [S_LEN, 1024], BF)
    grT = dscr("grT", [8, 128, S_LEN], BF)
    gaT = dscr("gaT", [8, 128, S_LEN], BF)
    ggT = dscr("ggT", [8, 128, S_LEN], BF)
    retT = dscr("retT", [8, 128, S_LEN], BF)
    x1d = dscr("x1d", [S_LEN, D], F32)
    h2T = dscr("h2T", [8, 128, S_LEN], BF)
    oTd = dscr("oTd", [4, 128, S_LEN], BF)

    import contextlib
    top = contextlib.ExitStack()

    def sb(stack, name, shape, dt):
        return stack.enter_context(nc.sbuf_tensor(name, list(shape), dt))

    with top:
        eng_sems = {e: top.enter_context(nc.semaphore("s_" + e)) for e in ("act", "dve", "pool", "pe")}
        dma_sems = {q: [top.enter_context(nc.semaphore(f"d_{q}{k}")) for k in range(Sched.NSLOT)]
                    for q in ("sp", "pool")}
        S = Sched(nc, eng_sems, dma_sems)
        psM = [top.enter_context(nc.psum_tensor(f"psM{i}", [128, 512], F32)) for i in range(6)]
        psT = [top.enter_context(nc.psum_tensor(f"psT{i}", [128, 1024], BF)) for i in range(2)]
        cb = sb(top, "cb_sb", [128, CB_W], BF)
        cf = sb(top, "cf_sb", [128, CF_W], F32)
        junk = sb(top, "junk", [128, D], F32)
        ident = cb[:, CB_ID:CB_ID + 128]
        rotA = cb[:, CB_ROTA:CB_ROTA + 128]
        rotR = cb[:, CB_ROTR:CB_ROTR + 128]
        maskb = cb[:, CB_MASK:CB_MASK + 256]
        onesz = [cb[:, CB_ONE0:CB_ONE0 + 128], cb[:, CB_ONE1:CB_ONE1 + 128]]

        def mm(ps, lhsT, rhs, start, stop):
            return lambda e: e.matmul(ps, lhsT=lhsT, rhs=rhs, start=start, stop=stop)

        def rmsnorm_chunk(xin_ap, xin_res, g_ap, h_out, h_res, ss, sd, rs, col, tag):
            S.add("act", lambda e: e.activation(out=junk[:], in_=xin_ap, func=AF.Square,
                                                accum_out=ss[:, col:col + 1]),
                  reads=[xin_res, "junk"], writes=[("ss", tag, col), "junk"])
            S.add("act", lambda e: e.activation(out=sd[:, col:col + 1], in_=ss[:, col:col + 1],
                                                func=AF.Sqrt, bias=epsb, scale=1.0 / D),
                  reads=[("ss", tag, col), "eps"], writes=[("sd", tag, col)])
            S.add("dve", lambda e: e.reciprocal(out=rs[:, col:col + 1], in_=sd[:, col:col + 1]),
                  reads=[("sd", tag, col)], writes=[("rs", tag, col)])
            S.add("dve", lambda e: e.scalar_tensor_tensor(out=h_out, in0=xin_ap, scalar=rs[:, col:col + 1],
                                                          in1=g_ap, op0=ALU.mult, op1=ALU.mult),
                  reads=[xin_res, ("rs", tag, col), "gvec"], writes=[h_res])

        epst = sb(top, "epst", [128, 1], F32)
        epsb = epst[:, 0:1]

        hstack = contextlib.ExitStack()
        hT = sb(hstack, "hT", [128, 8, S_LEN], BF)
        Wc0 = sb(hstack, "Wc0", [128, 8, 512], BF)
        w_in_v = w_in.rearrange("(k p) c -> p k c", p=128)
        gtypes = ["aq"] * 3 + ["ak"] * 3 + ["av"] * 3 + ["rq", "rk", "rv", "rv"] + ["gr"] * 2 + ["ga"] * 2 + ["gg"] * 2
        for ph in _phase("A", phases):
            xin = [sb(ph, f"xin{i}", [128, D], F32) for i in range(4)]
            hb = [sb(ph, f"hb{i}", [128, D], BF) for i in range(4)]
            g1 = sb(ph, "g1", [128, 8], F32)
            ss = sb(ph, "ssA", [128, NCH], F32)
            sd = sb(ph, "sdA", [128, NCH], F32)
            rs = sb(ph, "rsA", [128, NCH], F32)
            cosT = sb(ph, "cosT", [128, S_LEN], F32)
            sinT = sb(ph, "sinT", [128, S_LEN], F32)
            Wb = [sb(ph, f"Wb{i}", [128, 8, 512], BF) for i in range(4)]
            Ub = [sb(ph, f"Ub{i}", [128, 512], BF) for i in range(2)]
            Vb_ = [sb(ph, f"Vb{i}", [128, 512], BF) for i in range(2)]
            stF = [sb(ph, f"stF{i}", [128, S_LEN], BF) for i in range(2)]
            stT = [sb(ph, f"stT{i}", [128, 512], BF) for i in range(3)]
            ktm = sb(ph, "ktm", [128, NCH, 128], BF)

            S.add("dve", lambda e: e.memset(epst[:], EPS), writes=["eps"])
            S.add("pool", lambda e: e.dma_start(out=cb[:], in_=cbd[:, :]), writes=["cb"], dma=True)
            S.add("sp", lambda e: e.dma_start(out=cf[:], in_=cfd[:, :]), writes=["cf"], dma=True)
            S.add("sp", lambda e: e.dma_start(out=g1[:], in_=g1Td[:, :]), writes=["gvec"], dma=True)

            w_in_v = w_in.rearrange("(k p) c -> p k c", p=128)

            order = [6, 7, 8, 0, 1, 2, 3, 4, 5, 11, 12, 9, 10, 13, 14]

            def load_w(pos):
                wg = order[pos]
                buf = Wb[pos % 4]
                for k0 in range(0, 8, 4):
                    S.add("pool", lambda e, buf=buf, k0=k0, wg=wg: e.dma_start(
                        out=buf[:, k0:k0 + 4, :], in_=w_in_v[:, k0:k0 + 4, wg * 512:(wg + 1) * 512]),
                        writes=[("W", pos % 4)], dma=True)

            for pos_ in range(4):
                load_w(pos_)
            S.add("sp", lambda e: e.dma_start(out=cosT[:], in_=cosAd[:, :]), writes=["tab"], dma=True)
            S.add("sp", lambda e: e.dma_start(out=sinT[:], in_=sinAd[:, :]), writes=["tab"], dma=True)

            pm = [0]
            stc = [0]

            def tokmajor_chunk(pos, c, hres):
                wg = order[pos]
                W = Wb[pos % 4]
                wres = ("W", pos % 4)
                if gtypes[wg] == "av":
                    dst, c0 = vA, (wg - 6) * 512
                else:
                    dst, c0 = vR, (wg - 11) * 512
                ps = psM[pm[0] % 4]
                pres = ("psM", pm[0] % 4)
                pm[0] += 1
                for kc in range(8):
                    S.add("pe", mm(ps[:], hT[:, kc, c * 128:(c + 1) * 128], W[:, kc, :], kc == 0, kc == 7),
                          reads=[wres] + hres, writes=[pres])
                si = stc[0] % 3
                stc[0] += 1
                if stc[0] % 2 == 0 and not hres:
                    S.add("act", lambda e: e.activation(out=stT[si][:], in_=ps[:], func=AF.Copy),
                          reads=[pres], writes=[("stT", si)])
                else:
                    S.add("dve", lambda e: e.tensor_copy(out=stT[si][:], in_=ps[:]),
                          reads=[pres], writes=[("stT", si)])
                S.add("pool", lambda e: e.dma_start(out=dst[c * 128:(c + 1) * 128, c0:c0 + 512], in_=stT[si][:]),
                      reads=[("stT", si)], writes=["dscr"], dma=True)

            gtypes = ["aq"] * 3 + ["ak"] * 3 + ["av"] * 3 + ["rq", "rk", "rv", "rv"] + ["gr"] * 2 + ["ga"] * 2 + ["gg"] * 2

            def a1_front(c):
                b = c % 4
                S.add("sp", lambda e: e.dma_start(out=xin[b][:], in_=x[c * 128:(c + 1) * 128, :]),
                      writes=[("xin", b)], dma=True)
                S.add("act", lambda e: e.activation(out=junk[:], in_=xin[b][:], func=AF.Square,
                                                    accum_out=ss[:, c:c + 1]),
                      reads=[("xin", b), "junk"], writes=[("ss", c), "junk"])
                S.add("act", lambda e: e.activation(out=sd[:, c:c + 1], in_=ss[:, c:c + 1],
                                                    func=AF.Sqrt, bias=epsb, scale=1.0 / D),
                      reads=[("ss", c), "eps"], writes=[("sd", c)])
                S.add("dve", lambda e: e.reciprocal(out=rs[:, c:c + 1], in_=sd[:, c:c + 1]),
                      reads=[("sd", c)], writes=[("rs", c)])

            def a1_rest(c):
                b = c % 4
                pb = c % 2
                S.add("act", lambda e: e.activation(out=hb[b][:], in_=xin[b][:], func=AF.Copy,
                                                    scale=rs[:, c:c + 1]),
                      reads=[("xin", b), ("rs", c)], writes=[("hb", b)])
                for k in range(8):
                    S.add("pe", lambda e, k=k: e.transpose(psT[pb][:, k * 128:(k + 1) * 128],
                                                           hb[b][:, k * 128:(k + 1) * 128], ident),
                          reads=[("hb", b), "cb"], writes=[("psT", pb)])
                S.add("dve", lambda e: e.tensor_tensor(
                    out=hT[:, :, c * 128:(c + 1) * 128],
                    in0=psT[pb][:].rearrange("p (k t) -> p k t", k=8),
                    in1=g1[:, :].unsqueeze(2).broadcast_to([128, 8, 128]), op=ALU.mult),
                    reads=[("psT", pb), "gvec"], writes=[("hT", c)])

            for c in range(NCH + 2):
                if c < NCH:
                    a1_front(c)
                if 1 <= c <= NCH:
                    a1_rest(c - 1)
                if c >= 2:
                    for pos_ in range(3):
                        tokmajor_chunk(pos_, c - 2, [("hT", c - 2)])

            p2 = [0]
            sf = [0]
            tpc = [0]
            pending = []
            for pos in range(3, len(order)):
                wg = order[pos]
                ty = gtypes[wg]
                if ty in ("av", "rv", "gr"):
                    for f_ in pending:
                        f_()
                    pending.clear()
                if 4 <= pos + 1 < len(order):
                    load_w(pos + 1)
                if pos == len(order) - 2:
                    for k0 in range(0, 8, 4):
                        S.add("pool", lambda e, k0=k0: e.dma_start(
                            out=Wc0[:, k0:k0 + 4, :], in_=w_in_v[:, k0:k0 + 4, 15 * 512:16 * 512]),
                            writes=["Wc0"], dma=True)
                if wg == 11:
                    S.add("sp", lambda e: e.dma_start(out=cosT[:], in_=cosRd[:, :]), writes=["tab"], dma=True)
                    S.add("sp", lambda e: e.dma_start(out=sinT[:], in_=sinRd[:, :]), writes=["tab"], dma=True)
                W = Wb[pos % 4]
                wres = ("W", pos % 4)
                if ty in ("av", "rv"):
                    for c in range(NCH):
                        tokmajor_chunk(pos, c, [])
                    continue
                for j in range(4):
                    stage = stF[sf[0] % 2]
                    sres = ("stF", sf[0] % 2)
                    sf[0] += 1
                    if ty == "aq":
                        dstc = qkA[wg * 4 + j]
                    elif ty == "ak":
                        dstc = qkA[12 + (wg - 3) * 4 + j]
                    elif ty == "rq":
                        dstc = qkR[j]
                    elif ty == "rk":
                        dstc = qkR[4 + j]
                    elif ty == "gr":
                        dstc = grT[(wg - 13) * 4 + j]
                    elif ty == "ga":
                        dstc = gaT[(wg - 15) * 4 + j]
                    else:
                        dstc = ggT[(wg - 17) * 4 + j]
                    for tt in range(NTT):
                        tsl = slice(tt * 512, (tt + 1) * 512)
                        ps = psM[pm[0] % 4]
                        pres = ("psM", pm[0] % 4)
                        pm[0] += 1
                        for kc in range(8):
                            S.add("pe", mm(ps[:], W[:, kc, j * 128:(j + 1) * 128], hT[:, kc, tsl], kc == 0, kc == 7),
                                  reads=[wres], writes=[pres])
                        if ty in ("aq", "ak", "rq", "rk"):
                            ub = p2[0] % 2
                            p2[0] += 1
                            rot = rotA if ty in ("aq", "ak") else rotR
                            S.add("dve", lambda e, ps=ps, ub=ub, tsl=tsl: e.tensor_tensor(
                                out=Ub[ub][:], in0=ps[:], in1=cosT[:, tsl], op=ALU.mult),
                                reads=[pres, "tab"], writes=[("U", ub)])
                            S.add("dve", lambda e, ps=ps, ub=ub, tsl=tsl: e.tensor_tensor(
                                out=Vb_[ub][:], in0=ps[:], in1=sinT[:, tsl], op=ALU.mult),
                                reads=[pres, "tab"], writes=[("V", ub)])
                            sc = RET_SCALE if ty == "rk" else 1.0
                            last = (tt == NTT - 1)

                            dil = GROUP_DIL[wg] if ty == "aq" else (GROUP_DIL[wg - 3] if ty == "ak" else 1)

                            def finish(ub=ub, rot=rot, stage=stage, sres=sres, tsl=tsl, sc=sc, ty=ty, tt=tt, j=j,
                                       last=last, dstc=dstc, dil=dil):
                                ps2 = psM[4 + ub]
                                S.add("pe", mm(ps2[:], ident, Ub[ub][:], True, False),
                                      reads=[("U", ub), "cb"], writes=[("psM", 4 + ub)])
                                S.add("pe", mm(ps2[:], rot, Vb_[ub][:], False, True),
                                      reads=[("V", ub), "cb"], writes=[("psM", 4 + ub)])
                                if dil > 1:
                                    w = 512 // dil
                                    o_ap = stage[:].rearrange("p (r l) -> p r l", r=dil)[:, :, tt * w:(tt + 1) * w]
                                    i_ap = ps2[:].rearrange("p (i r) -> p r i", r=dil)
                                else:
                                    o_ap = stage[:, tsl]
                                    i_ap = ps2[:]
                                S.add("act", lambda e: e.activation(out=o_ap, in_=i_ap, func=AF.Copy, scale=sc),
                                      reads=[("psM", 4 + ub)], writes=[sres])
                                if ty == "rk":
                                    tb = tpc[0] % 2
                                    tpc[0] += 1
                                    for c in range(4):
                                        S.add("pe", lambda e, c=c: e.transpose(
                                            psT[tb][:, c * 128:(c + 1) * 128],
                                            stage[:, tt * 512 + c * 128: tt * 512 + (c + 1) * 128], ident),
                                            reads=[sres, "cb"], writes=[("psT", tb)])
                                    S.add("dve", lambda e: e.tensor_copy(
                                        out=ktm[:, tt * 4:(tt + 1) * 4, :],
                                        in_=psT[tb][:, 0:512].rearrange("p (c d) -> p c d", c=4)),
                                        reads=[("psT", tb)], writes=["ktm"])
                                if last:
                                    S.add("pool", lambda e: e.dma_start(out=dstc[:, :], in_=stage[:]),
                                          reads=[sres], writes=["dscr"], dma=True)
                                    if ty == "rk":
                                        kv_ = kRt.rearrange("(n j) c -> j n c", j=128)
                                        for n0 in range(0, NCH, 8):
                                            S.add("pool", lambda e, n0=n0: e.dma_start(
                                                out=kv_[:, n0:n0 + 8, j * 128:(j + 1) * 128], in_=ktm[:, n0:n0 + 8, :]),
                                                reads=["ktm"], writes=["dscr"], dma=True)

                            for f_ in pending:
                                f_()
                            pending.clear()
                            pending.append(finish)
                        else:
                            fn = AF.Silu if ty == "gr" else AF.Sigmoid
                            S.add("act", lambda e, ps=ps, stage=stage, tsl=tsl, fn=fn: e.activation(
                                out=stage[:, tsl], in_=ps[:], func=fn),
                                reads=[pres], writes=[sres])
                            if tt == NTT - 1:
                                S.add("pool", lambda e, stage=stage, dstc=dstc: e.dma_start(out=dstc[:, :], in_=stage[:]),
                                      reads=[sres], writes=["dscr"], dma=True)
            for f_ in pending:
                f_()
            pending.clear()
            S.barrier()
            S.emit()

        for ph in _phase("C", phases):
            Wb2 = [Wc0] + [sb(ph, f"Wc{i}", [128, 8, 512], BF) for i in range(1, 3)]
            stF2 = [sb(ph, f"stC{i}", [128, S_LEN], BF) for i in range(2)]
            qT = sb(ph, "rqT", [128, S_LEN], BF)
            kT = sb(ph, "rkT", [128, S_LEN], BF)
            Kt = sb(ph, "rKt", [128, NCH, 128], BF)
            Vr = sb(ph, "rV", [128, NCH, 256], BF)
            gTb = [sb(ph, f"rgT{i}", [128, 2, 1024], BF) for i in range(2)]
            qxi = sb(ph, "rqxi", [128, NCH, 128], BF)
            Sf = [sb(ph, f"rSf{i}", [128, 256], F32) for i in range(2)]
            Sbr = [sb(ph, f"rSb{i}", [128, 256], BF) for i in range(6)]
            AD = [sb(ph, f"rAD{i}", [128, 128], BF) for i in range(4)]
            osb = sb(ph, "rosb", [128, NCH, 256], BF)
            s1 = sb(ph, "rs1", [128, NCH], F32)
            s2 = sb(ph, "rs2", [128, NCH], F32)
            mean = sb(ph, "rmean", [128, NCH], F32)
            msq = sb(ph, "rmsq", [128, NCH], F32)
            var = sb(ph, "rvar", [128, NCH], F32)
            vstd = sb(ph, "rvstd", [128, NCH], F32)
            vrs = sb(ph, "rvrs", [128, NCH], F32)
            rstb = [sb(ph, f"rst{i}", [128, 2, 1024], BF) for i in range(2)]
            gam = [float(np.exp(np.log1p(-(2.0 ** (-5.0 - h))))) for h in range(4)]
            kv_ = kRt.rearrange("(n j) c -> j n c", j=128)
            vv_ = vR.rearrange("(n j) c -> j n c", j=128)
            gorder = [15, 16, 17, 18]

            def load_wc(pos):
                wg = gorder[pos]
                for k0 in range(0, 8, 4):
                    S.add("pool", lambda e, k0=k0: e.dma_start(
                        out=Wb2[pos % 3][:, k0:k0 + 4, :], in_=w_in_v[:, k0:k0 + 4, wg * 512:(wg + 1) * 512]),
                        writes=[("Wc", pos % 3)], dma=True)

            def loadC(h):
                S.add("sp", lambda e: e.dma_start(out=qT[:], in_=qkR[h][:, :]), writes=["qT"], dma=True)
                S.add("sp", lambda e: e.dma_start(out=kT[:], in_=qkR[4 + h][:, :]), writes=["kT"], dma=True)
                for n0 in range(0, NCH, 8):
                    S.add("sp", lambda e, n0=n0: e.dma_start(
                        out=Kt[:, n0:n0 + 8, :], in_=kv_[:, n0:n0 + 8, h * 128:(h + 1) * 128]),
                        writes=["Kt"], dma=True)
                    S.add("sp", lambda e, n0=n0: e.dma_start(
                        out=Vr[:, n0:n0 + 8, :], in_=vv_[:, n0:n0 + 8, h * 256:(h + 1) * 256]),
                        writes=["Vr"], dma=True)

            SK = 3

            def load_g(h, blk):
                for a in range(2):
                    S.add("sp", lambda e, a=a: e.dma_start(
                        out=gTb[blk % 2][:, a, :], in_=grT[2 * h + a][:, blk * 1024:(blk + 1) * 1024]),
                        writes=[("gT", blk % 2)], dma=True)

            def retention_steps():
                gi = [0]
                for h in range(4):
                    cd = gam[h] ** 128
                    if h == 0:
                        loadC(0)
                        S.add("dve", lambda e: e.tensor_scalar(
                            out=Kt[:], in0=Kt[:], scalar1=cf[:, CF_ZETA:CF_ZETA + 1], scalar2=None, op0=ALU.mult),
                            reads=["Kt", "cf"], writes=["Kt"])
                        S.add("dve", lambda e: e.tensor_tensor(
                            out=qxi[:], in0=qT[:].rearrange("p (n i) -> p n i", n=NCH),
                            in1=cf[:, CF_XI:CF_XI + 128].unsqueeze(1).broadcast_to([128, NCH, 128]),
                            op=ALU.mult), reads=["qT", "cf"], writes=["qxi"])
                    yield
                    for n in range(NCH + SK):
                        if n == 4:
                            load_g(h, 0)
                        if n < NCH:
                            csl = slice(n * 128, (n + 1) * 128)
                            bank = psM[2 + n % 2]
                            bres = ("psM", 2 + n % 2)
                            kvp = bank[:, 0:256]
                            atp = bank[:, 256:384]
                            S.add("pe", mm(kvp, Kt[:, n, :], Vr[:, n, :], True, True),
                                  reads=["Kt", "Vr"], writes=[bres])
                            S.add("pe", mm(atp, kT[:, csl], qT[:, csl], True, True),
                                  reads=["kT", "qT"], writes=[bres])
                            if n == 0:
                                S.add("dve", lambda e, kvp=kvp: e.tensor_copy(out=Sf[0][:], in_=kvp),
                                      reads=[bres], writes=[("Sf", 0)])
                            else:
                                S.add("dve", lambda e, kvp=kvp, n=n, cd=cd: e.scalar_tensor_tensor(
                                    out=Sf[n % 2][:], in0=Sf[(n - 1) % 2][:], scalar=cd, in1=kvp,
                                    op0=ALU.mult, op1=ALU.add),
                                    reads=[bres, ("Sf", (n - 1) % 2)], writes=[("Sf", n % 2)])
                            if n < NCH - 1:
                                S.add("dve", lambda e, n=n: e.tensor_copy(out=Sbr[(n + 1) % 6][:], in_=Sf[n % 2][:]),
                                      reads=[("Sf", n % 2)], writes=[("Sb", (n + 1) % 6)])
                            S.add("dve", lambda e, atp=atp, n=n, h=h: e.tensor_tensor(
                                out=AD[n % 4][:], in0=atp, in1=cf[:, CF_DEC + h * 128:CF_DEC + (h + 1) * 128],
                                op=ALU.mult), reads=[bres, "cf"], writes=[("AD", n % 4)])
                        if n >= SK:
                            m = n - SK
                            op_ = psM[4 + m % 2]
                            S.add("pe", mm(op_[:, 0:256], AD[m % 4][:], Vr[:, m, :], True, m == 0),
                                  reads=[("AD", m % 4), "Vr"], writes=[("psM", 4 + m % 2)])
                            if m > 0:
                                S.add("pe", mm(op_[:, 0:256], qxi[:, m, :], Sbr[m % 6][:], False, True),
                                      reads=["qxi", ("Sb", m % 6)], writes=[("psM", 4 + m % 2)])
                            S.add("act", lambda e, op_=op_, m=m: e.activation(out=osb[:, m, :], in_=op_[:, 0:256],
                                                                              func=AF.Copy, accum_out=s1[:, m:m + 1]),
                                  reads=[("psM", 4 + m % 2)], writes=["osb", "s1"])
                            S.add("act", lambda e, op_=op_, m=m: e.activation(out=junk[:, 0:256], in_=op_[:, 0:256],
                                                                              func=AF.Square, accum_out=s2[:, m:m + 1]),
                                  reads=[("psM", 4 + m % 2), "junk"], writes=["s2", "junk"])
                        yield
                    load_g(h, 1)
                    if h + 1 < 4:
                        loadC(h + 1)
                    S.add("dve", lambda e: e.tensor_scalar(out=mean[:], in0=s1[:], scalar1=1.0 / 256, scalar2=None,
                                                           op0=ALU.mult), reads=["s1"], writes=["mean"])
                    S.add("dve", lambda e: e.tensor_tensor(out=msq[:], in0=mean[:], in1=mean[:], op=ALU.mult),
                          reads=["mean"], writes=["msq"])
                    S.add("dve", lambda e: e.scalar_tensor_tensor(out=var[:], in0=s2[:], scalar=1.0 / 256, in1=msq[:],
                                                                  op0=ALU.mult, op1=ALU.subtract),
                          reads=["s2", "msq"], writes=["var"])
                    S.add("act", lambda e: e.activation(out=vstd[:], in_=var[:], func=AF.Sqrt, bias=epsb, scale=1.0),
                          reads=["var"], writes=["vstd"])
                    S.add("dve", lambda e: e.reciprocal(out=vrs[:], in_=vstd[:]), reads=["vstd"], writes=["vrs"])
                    S.add("dve", lambda e: e.scalar_tensor_tensor(out=msq[:], in0=mean[:], scalar=-1.0, in1=vrs[:],
                                                                  op0=ALU.mult, op1=ALU.mult),
                          reads=["mean", "vrs"], writes=["nb"])
                    for _ in range(10):
                        yield

                    def normalise(n):
                        S.add("act", lambda e: e.activation(
                            out=osb[:, n, :], in_=osb[:, n, :], func=AF.Identity,
                            scale=vrs[:, n:n + 1], bias=msq[:, n:n + 1]),
                            reads=["osb", "nb", "vrs"], writes=[("osbn", n)])
                    normalise(0)
                    normalise(1)
                    yield
                    yield
                    for blk in range(4):
                        gb = blk % 2
                        bsl = slice(blk * 1024, (blk + 1) * 1024)
                        if 1 <= blk and blk + 1 < 4:
                            load_g(h, blk + 1)
                        for nn in range(8):
                            n = blk * 8 + nn
                            if n + 2 < NCH:
                                normalise(n + 2)
                            tb = n % 2
                            for a in range(2):
                                S.add("pe", lambda e, n=n, a=a, tb=tb: e.transpose(
                                    psT[tb][:, a * 128:(a + 1) * 128], osb[:, n, a * 128:(a + 1) * 128], ident),
                                    reads=[("osbn", n), "cb"], writes=[("psT", tb)])
                            S.add("dve", lambda e, nn=nn, tb=tb, gb=gb: e.tensor_tensor(
                                out=rstb[gb][:, :, nn * 128:(nn + 1) * 128],
                                in0=psT[tb][:, 0:256].rearrange("p (a t) -> p a t", a=2),
                                in1=gTb[gb][:, :, nn * 128:(nn + 1) * 128], op=ALU.mult),
                                reads=[("psT", tb), ("gT", gb)], writes=[("rst", gb)])
                            if n == 12 and h + 1 < 4:
                                S.add("dve", lambda e, h=h: e.tensor_scalar(
                                    out=Kt[:], in0=Kt[:], scalar1=cf[:, CF_ZETA + h + 1:CF_ZETA + h + 2], scalar2=None,
                                    op0=ALU.mult), reads=["Kt", "cf"], writes=["Kt"])
                                S.add("dve", lambda e, h=h: e.tensor_tensor(
                                    out=qxi[:], in0=qT[:].rearrange("p (n i) -> p n i", n=NCH),
                                    in1=cf[:, CF_XI + (h + 1) * 128:CF_XI + (h + 2) * 128].unsqueeze(1).broadcast_to(
                                        [128, NCH, 128]),
                                    op=ALU.mult), reads=["qT", "cf"], writes=["qxi"])
                            yield
                        for a in range(2):
                            S.add("pool", lambda e, h=h, a=a, gb=gb, bsl=bsl: e.dma_start(
                                out=retT[2 * h + a][:, bsl], in_=rstb[gb][:, a, :]),
                                reads=[("rst", gb)], writes=["dscr"], dma=True)

            ret = retention_steps()

            def adv(k):
                for _ in range(k):
                    try:
                        next(ret)
                    except StopIteration:
                        return

            load_wc(1)
            adv(1)
            pmc = [0]
            sfc = [0]
            git = [0]
            for pos in range(4):
                wg = gorder[pos]
                if pos + 2 < 4:
                    load_wc(pos + 2)
                W = Wb2[pos % 3]
                wres = ("Wc", pos % 3)
                for j in range(4):
                    stage = stF2[sfc[0] % 2]
                    sres = ("stC", sfc[0] % 2)
                    sfc[0] += 1
                    dstc = gaT[(wg - 15) * 4 + j] if wg < 17 else ggT[(wg - 17) * 4 + j]
                    for tt in range(NTT):
                        tsl = slice(tt * 512, (tt + 1) * 512)
                        ps = psM[pmc[0] % 2]
                        pres = ("psM", pmc[0] % 2)
                        pmc[0] += 1
                        for kc in range(8):
                            S.add("pe", mm(ps[:], W[:, kc, j * 128:(j + 1) * 128], hT[:, kc, tsl], kc == 0, kc == 7),
                                  reads=[wres], writes=[pres])
                            if kc == 3 and git[0] >= 10:
                                adv(1)
                        S.add("act", lambda e, ps=ps, stage=stage, tsl=tsl: e.activation(
                            out=stage[:, tsl], in_=ps[:], func=AF.Sigmoid), reads=[pres], writes=[sres])
                        if git[0] >= 64:
                            adv(2)
                        elif git[0] >= 10:
                            adv(2 if git[0] % 2 == 0 else 1)
                        git[0] += 1
                    S.add("pool", lambda e, stage=stage, dstc=dstc: e.dma_start(out=dstc[:, :], in_=stage[:]),
                          reads=[sres], writes=["dscr"], dma=True)
            adv(10 ** 6)
            S.barrier()
            S.emit()
        hstack.close()

        bd = contextlib.ExitStack()
        Woa = sb(bd, "Woa", [128, 4, D], BF)
        Wor = sb(bd, "Wor", [128, 8, D], BF)
        Wo = sb(bd, "Wo", [128, 8, D], BF)
        for ph in _phase("B", phases):
            qTb = [sb(ph, f"qTb{i}", [128, S_LEN], BF) for i in range(2)]
            kTz = [[sb(ph, f"kTz{i}{e}", [128, S_LEN], BF) for e in range(2)] for i in range(2)]
            Vz = [[sb(ph, f"Vz{i}{e}", [128, NCH, 128], BF) for e in range(2)] for i in range(2)]
            NUMLs = [sb(ph, f"NUML{i}", [128, 2, S_LEN], F32) for i in range(2)]
            Pb = [sb(ph, f"Pb{i}", [128, 512], BF) for i in range(6)]
            rec = [sb(ph, f"rec{i}", [128, 512], F32) for i in range(1)]
            lnb = [sb(ph, f"lnb{i}", [128, 512], F32) for i in range(1)]
            oTs = [sb(ph, f"oTs{i}", [128, 512], BF) for i in range(2)]

            for i in range(2):
                S.add("pool", lambda e, i=i: e.memset(kTz[i][0][64:128, :], 0.0), writes=[("kTz", i)])
                S.add("pool", lambda e, i=i: e.memset(kTz[i][1][0:64, :], 0.0), writes=[("kTz", i)])
                S.add("pool", lambda e, i=i: e.memset(Vz[i][0][:, :, 64:128], 1.0), writes=[("Vz", i)])
                S.add("pool", lambda e, i=i: e.memset(Vz[i][1][:, :, 0:64], 1.0), writes=[("Vz", i)])

            units = [(hp, g) for hp in range(4) for g in range(3)]

            def load_unit(ui):
                hp, g = units[ui]
                b = ui % 2
                d = GROUP_DIL[g]
                nbk = NCH // d
                uq = g * 4 + hp
                uk = 12 + g * 4 + hp
                S.add("sp", lambda e: e.dma_start(out=qTb[b][:], in_=qkA[uq][:, :]), writes=[("qT", b)], dma=True)
                S.add("sp", lambda e: e.dma_start(out=kTz[b][0][0:64, :], in_=qkA[uk][0:64, :]),
                      writes=[("kTz", b)], dma=True)
                S.add("sp", lambda e: e.dma_start(out=kTz[b][1][64:128, :], in_=qkA[uk][64:128, :]),
                      writes=[("kTz", b)], dma=True)
                vv = vA.rearrange("(n j r) c -> j r n c", j=128, r=d)
                for e_ in range(2):
                    c0 = g * 512 + hp * 128 + e_ * 64
                    for kb0 in range(0, NCH, 8):
                        dst = Vz[b][e_][:, kb0:kb0 + 8, e_ * 64:(e_ + 1) * 64]
                        if nbk >= 8:
                            r, n0 = divmod(kb0, nbk)
                            src = vv[:, r, n0:n0 + 8, c0:c0 + 64]
                        else:
                            r0 = kb0 // nbk
                            nr = 8 // nbk
                            for rr in range(nr):
                                src = vv[:, r0 + rr, :, c0:c0 + 64]
                                dst = Vz[b][e_][:, kb0 + rr * nbk:kb0 + (rr + 1) * nbk, e_ * 64:(e_ + 1) * 64]
                                S.add("sp", lambda e, dst=dst, src=src: e.dma_start(out=dst, in_=src),
                                      writes=[("Vz", b)], dma=True)
                            continue
                        S.add("sp", lambda e, dst=dst, src=src: e.dma_start(out=dst, in_=src),
                              writes=[("Vz", b)], dma=True)

            steps = [(ui, kb) for ui in range(len(units)) for kb in range(NCH)]

            def pos(d, kb):
                nbk = NCH // d
                r, n = divmod(kb, nbk)
                return r, n, nbk, d * 128 * n + r

            def emit_S(si):
                ui, kb = steps[si]
                hp, g = units[ui]
                b = ui % 2
                d = GROUP_DIL[g]
                r, n, nbk, st = pos(d, kb)
                nq = 256 if n < nbk - 1 else 128
                ps = psM[si % 3]
                c0 = r * (S_LEN // d) + 128 * n
                kcols = slice(c0, c0 + 128)
                qcols = slice(c0, c0 + nq)
                for e_ in range(2):
                    S.add("pe", mm(ps[:, e_ * 256:e_ * 256 + nq], kTz[b][e_][:, kcols], qTb[b][:, qcols], True, False),
                          reads=[("kTz", b), ("qT", b)], writes=[("psM", si % 3)])
                    S.add("pe", mm(ps[:, e_ * 256:e_ * 256 + nq], ident, maskb[:, 0:nq], False, True),
                          reads=["cb"], writes=[("psM", si % 3)])
                P = Pb[si % 6]
                S.add("act", lambda e: e.activation(
                    out=P[:].rearrange("p (a q) -> p a q", a=2)[:, :, 0:nq],
                    in_=ps[:].rearrange("p (a q) -> p a q", a=2)[:, :, 0:nq],
                    func=AF.Exp, scale=0.125),
                    reads=[("psM", si % 3)], writes=[("P", si % 6)])

            def emit_PV(si):
                ui, kb = steps[si]
                hp, g = units[ui]
                b = ui % 2
                d = GROUP_DIL[g]
                r, n, nbk, st = pos(d, kb)
                pv = psM[3 + si % 2]
                pres = ("psM", 3 + si % 2)
                P = Pb[si % 6]
                Pp = Pb[(si - 1) % 6]
                NUML = NUMLs[hp % 2]
                for e_ in range(2):
                    terms = []
                    if n > 0:
                        terms.append((Vz[b][e_][:, kb - 1, :], Pp[:, e_ * 256 + 128:e_ * 256 + 256], ("P", (si - 1) % 6)))
                    terms.append((Vz[b][e_][:, kb, :], P[:, e_ * 256:e_ * 256 + 128], ("P", si % 6)))
                    for ti, (lh, rh, rres) in enumerate(terms):
                        S.add("pe", mm(pv[:, e_ * 128:(e_ + 1) * 128], lh, rh, ti == 0, ti == len(terms) - 1),
                              reads=[rres, ("Vz", b)], writes=[pres])
                dst = NUML[:, :, st:st + d * 127 + 1:d]
                src = pv[:, 0:256].rearrange("p (a q) -> p a q", a=2)
                if g == 0:
                    S.add("dve", lambda e: e.tensor_copy(out=dst, in_=src), reads=[pres], writes=[("numl", hp)])
                else:
                    S.add("dve", lambda e: e.tensor_tensor(out=dst, in0=src, in1=dst, op=ALU.add),
                          reads=[pres], writes=[("numl", hp)])
                if g == 2 and kb == NCH - 1:
                    for q8 in range(8):
                        sl = slice(q8 * 512, (q8 + 1) * 512)

                        def item(sl=sl, hp=hp, NUML=NUML, q8=q8):
                            rb = 0
                            ob = q8 % 2
                            S.add("act", lambda e: e.activation(out=lnb[rb][0:64, :], in_=NUML[64:128, 0, sl], func=AF.Ln),
                                  reads=[("numl", hp)], writes=[("lnb", rb)])
                            S.add("act", lambda e: e.activation(out=lnb[rb][64:128, :], in_=NUML[0:64, 1, sl], func=AF.Ln),
                                  reads=[("numl", hp)], writes=[("lnb", rb)])
                            S.add("act", lambda e: e.activation(out=rec[rb][:], in_=lnb[rb][:], func=AF.Exp, scale=-1.0),
                                  reads=[("lnb", rb)], writes=[("rec", rb)])
                            S.add("dve", lambda e: e.tensor_tensor(out=oTs[ob][0:64, :], in0=NUML[0:64, 0, sl],
                                                                   in1=rec[rb][0:64, :], op=ALU.mult),
                                  reads=[("rec", rb), ("numl", hp)], writes=[("oTs", ob)])
                            S.add("dve", lambda e: e.tensor_tensor(out=oTs[ob][64:128, :], in0=NUML[64:128, 1, sl],
                                                                   in1=rec[rb][64:128, :], op=ALU.mult),
                                  reads=[("rec", rb), ("numl", hp)], writes=[("oTs", ob)])
                            S.add("sp", lambda e: e.dma_start(out=oTd[hp][:, sl], in_=oTs[ob][:]),
                                  reads=[("oTs", ob)], writes=["dscr"], dma=True)
                        deferred.append(item)

            deferred = []
            load_unit(0)
            S.add("pool", lambda e: e.dma_start(out=Woa[:], in_=w_oa.rearrange("(k p) c -> p k c", p=128)),
                  writes=["Woa"], dma=True)
            for k0 in range(0, 8, 4):
                S.add("pool", lambda e, k0=k0: e.dma_start(
                    out=Wor[:, k0:k0 + 4, :], in_=w_or.rearrange("(k p) c -> p k c", p=128)[:, k0:k0 + 4, :]),
                    writes=["Wor"], dma=True)
            for k0 in range(0, 8, 4):
                S.add("pool", lambda e, k0=k0: e.dma_start(
                    out=Wo[:, k0:k0 + 4, :], in_=w_o.rearrange("(k p) c -> p k c", p=128)[:, k0:k0 + 4, :]),
                    writes=["Wo"], dma=True)
            NS = len(steps)
            for si in range(NS + 2):
                if si < NS:
                    emit_S(si)
                if si >= 2:
                    emit_PV(si - 2)
                    if deferred and si % 3 == 0:
                        deferred.pop(0)()
                if si % NCH == 1 and si // NCH + 1 < len(units):
                    load_unit(si // NCH + 1)
            while deferred:
                deferred.pop(0)()
            S.barrier()
            S.emit()

        for ph in _phase("D", phases):
            g2 = sb(ph, "g2", [128, D], F32)
            rtT = [sb(ph, f"rtT{i}", [128, 8, 512], BF) for i in range(2)]
            sgA = [sb(ph, f"sgA{i}", [128, 8, 512], BF) for i in range(2)]
            sgR = [sb(ph, f"sgR{i}", [128, 8, 512], BF) for i in range(2)]
            oTt = [sb(ph, f"oTt{i}", [128, 4, 512], BF) for i in range(2)]
            t1 = [sb(ph, f"t1{i}", [128, 512], F32) for i in range(2)]
            t2 = [sb(ph, f"t2{i}", [128, 512], F32) for i in range(2)]
            mgs = [sb(ph, f"mg{i}", [128, 8, 512], BF) for i in range(2)]
            xc = [sb(ph, f"xc{i}", [128, D], F32) for i in range(3)]
            x1c = [sb(ph, f"x1c{i}", [128, D], F32) for i in range(3)]
            h2b = [sb(ph, f"h2b{i}", [128, D], BF) for i in range(4)]
            h2s = [sb(ph, f"h2s{i}", [128, 8, 512], BF) for i in range(2)]
            ssD = sb(ph, "ssD", [128, NCH], F32)
            sdD = sb(ph, "sdD", [128, NCH], F32)
            rsD = sb(ph, "rsD", [128, NCH], F32)
            S.add("sp", lambda e: e.dma_start(out=g2[:], in_=g2d[:, :]), writes=["gvec"], dma=True)

            def loadD(tt):
                b = tt % 2
                tsl = slice(tt * 512, (tt + 1) * 512)
                S.add("sp", lambda e: e.dma_start(out=oTt[b][:], in_=oTd.rearrange("k p t -> p k t")[:, :, tsl]),
                      writes=[("oTt", b)], dma=True)
                S.add("sp", lambda e: e.dma_start(out=rtT[b][:], in_=retT.rearrange("k p t -> p k t")[:, :, tsl]),
                      writes=[("rtT", b)], dma=True)
                S.add("sp", lambda e: e.dma_start(out=sgA[b][:], in_=gaT.rearrange("k p t -> p k t")[:, :, tsl]),
                      writes=[("sgA", b)], dma=True)
                S.add("sp", lambda e: e.dma_start(out=sgR[b][:], in_=ggT.rearrange("k p t -> p k t")[:, :, tsl]),
                      writes=[("sgR", b)], dma=True)

            loadD(0)
            xi_ = [0]
            deferred = []
            for tt in range(NTT):
                b = tt % 2
                mg = mgs[b]
                mres = ("mg", b)
                tsl = slice(tt * 512, (tt + 1) * 512)
                for fo in range(8):
                    if fo == 4 and tt + 1 < NTT:
                        loadD(tt + 1)
                    fsl = slice(fo * 128, (fo + 1) * 128)
                    pa = psM[fo % 2]
                    pr = psM[2 + fo % 2]
                    for kc in range(4):
                        S.add("pe", mm(pa[:], Woa[:, kc, fsl], oTt[b][:, kc, :], kc == 0, kc == 3),
                              reads=["Woa", ("oTt", b)], writes=[("psM", fo % 2)])
                    for kc in range(8):
                        S.add("pe", mm(pr[:], Wor[:, kc, fsl], rtT[b][:, kc, :], kc == 0, kc == 7),
                              reads=["Wor", ("rtT", b)], writes=[("psM", 2 + fo % 2)])
                    tb = fo % 2
                    S.add("dve", lambda e, pa=pa, tb=tb, fo=fo, b=b: e.tensor_tensor(
                        out=t1[tb][:], in0=pa[:], in1=sgA[b][:, fo, :], op=ALU.mult),
                        reads=[("psM", fo % 2), ("sgA", b)], writes=[("t1", tb)])
                    S.add("dve", lambda e, pr=pr, tb=tb, fo=fo, b=b: e.tensor_tensor(
                        out=t2[tb][:], in0=pr[:], in1=sgR[b][:, fo, :], op=ALU.mult),
                        reads=[("psM", 2 + fo % 2), ("sgR", b)], writes=[("t2", tb)])
                    S.add("pool", lambda e, tb=tb, fo=fo, mg=mg: e.tensor_tensor(
                        out=mg[:, fo, :], in0=t1[tb][:], in1=t2[tb][:], op=ALU.add),
                        reads=[("t1", tb), ("t2", tb)], writes=[(mres, fo)])
                hs = h2s[tt % 2]
                for c in range(4):
                    cg = tt * 4 + c
                    xb = xi_[0] % 3
                    xi_[0] += 1
                    S.add("sp", lambda e, xb=xb, cg=cg: e.dma_start(out=xc[xb][:], in_=x[cg * 128:(cg + 1) * 128, :]),
                          writes=[("xc", xb)], dma=True)
                    for half in range(2):
                        pd = psM[4 + half]
                        for kc in range(8):
                            S.add("pe", mm(pd[:], mg[:, kc, c * 128:(c + 1) * 128], Wo[:, kc, half * 512:(half + 1) * 512],
                                           kc == 0, kc == 7), reads=[(mres, kc), "Wo"], writes=[("psM", 4 + half)])
                        S.add("dve", lambda e, pd=pd, xb=xb, half=half: e.tensor_tensor(
                            out=x1c[xb][:, half * 512:(half + 1) * 512], in0=pd[:],
                            in1=xc[xb][:, half * 512:(half + 1) * 512], op=ALU.add),
                            reads=[("psM", 4 + half), ("xc", xb)], writes=[("x1c", xb)])
                    S.add("pool", lambda e, xb=xb, cg=cg: e.dma_start(out=x1d[cg * 128:(cg + 1) * 128, :], in_=x1c[xb][:]),
                          reads=[("x1c", xb)], writes=["dscr"], dma=True)
                    hb_ = cg % 4
                    rmsnorm_chunk(x1c[xb][:], ("x1c", xb), g2[:], h2b[hb_][:], ("h2b", hb_), ssD, sdD, rsD, cg, "D")

                    def item(hb_=hb_, c=c, hs=hs, tt=tt, tsl=tsl, cg=cg):
                        pt = cg % 2
                        for k in range(8):
                            S.add("pe", lambda e, k=k: e.transpose(
                                psT[pt][:, k * 128:(k + 1) * 128], h2b[hb_][:, k * 128:(k + 1) * 128], ident),
                                reads=[("h2b", hb_), "cb"], writes=[("psT", pt)])
                        S.add("act", lambda e: e.activation(
                            out=hs[:, :, c * 128:(c + 1) * 128],
                            in_=psT[pt][:].rearrange("p (k t) -> p k t", k=8), func=AF.Copy),
                            reads=[("psT", pt)], writes=[("h2s", tt % 2)])
                        if c == 3:
                            S.add("pool", lambda e: e.dma_start(
                                out=h2T.rearrange("k p t -> p k t")[:, :, tsl], in_=hs[:]),
                                reads=[("h2s", tt % 2)], writes=["dscr"], dma=True)
                    deferred.append(item)
                    while len(deferred) > 2:
                        deferred.pop(0)()
            while deferred:
                deferred.pop(0)()
            S.barrier()
            S.emit()

        bd.close()

        for ph in _phase("E", phases):
            Wg = sb(ph, "Wg", [128, 8, FFN], BF)
            Wu = sb(ph, "Wu", [128, 8, FFN], BF)
            Wd = sb(ph, "Wd", [128, NJ, D], BF)
            gF = sb(ph, "gF", [128, D], F32)
            h2t = [sb(ph, f"h2t{i}", [128, 8, 512], BF) for i in range(2)]
            act = sb(ph, "actb", [128, NJ, 512], BF)
            sg = [sb(ph, f"sg{i}", [128, 512], BF) for i in range(2)]
            x1t = [sb(ph, f"x1t{i}", [128, D], F32) for i in range(2)]
            x2 = x1t
            yo = x1t
            ssE = sb(ph, "ssE", [128, NCH], F32)
            sdE = sb(ph, "sdE", [128, NCH], F32)
            rsE = sb(ph, "rsE", [128, NCH], F32)
            wgv = w_g.rearrange("(k p) c -> p k c", p=128)
            wuv = w_u.rearrange("(k p) c -> p k c", p=128)
            wdv = w_d.rearrange("(j p) c -> p j c", p=128)
            for jb in range(NJ // 2):
                csl = slice(jb * 256, (jb + 1) * 256)
                S.add("pool", lambda e, csl=csl: e.dma_start(out=Wg[:, :, csl], in_=wgv[:, :, csl]),
                      writes=[("Wg", jb)], dma=True)
                S.add("pool", lambda e, csl=csl: e.dma_start(out=Wu[:, :, csl], in_=wuv[:, :, csl]),
                      writes=[("Wu", jb)], dma=True)
            for j0 in range(0, NJ, 2):
                S.add("pool", lambda e, j0=j0: e.dma_start(out=Wd[:, j0:j0 + 2, :], in_=wdv[:, j0:j0 + 2, :]),
                      writes=[("Wd", j0 // 2)], dma=True)
            S.add("sp", lambda e: e.dma_start(out=gF[:], in_=gFd[:, :]), writes=["gvec"], dma=True)

            def loadE(tt):
                b = tt % 2
                tsl = slice(tt * 512, (tt + 1) * 512)
                S.add("sp", lambda e: e.dma_start(out=h2t[b][:], in_=h2T.rearrange("k p t -> p k t")[:, :, tsl]),
                      writes=[("h2t", b)], dma=True)

            loadE(0)
            for tt in range(NTT):
                b = tt % 2
                if tt + 1 < NTT:
                    loadE(tt + 1)
                for j in range(NJ):
                    jsl = slice(j * 128, (j + 1) * 128)
                    pg = psM[j % 2]
                    pu = psM[2 + j % 2]
                    for kc in range(8):
                        S.add("pe", mm(pg[:], Wg[:, kc, jsl], h2t[b][:, kc, :], kc == 0, kc == 7),
                              reads=[("Wg", j // 2), ("h2t", b)], writes=[("psM", j % 2)])
                    for kc in range(8):
                        S.add("pe", mm(pu[:], Wu[:, kc, jsl], h2t[b][:, kc, :], kc == 0, kc == 7),
                              reads=[("Wu", j // 2), ("h2t", b)], writes=[("psM", 2 + j % 2)])
                    sb_ = j % 2
                    S.add("act", lambda e, pg=pg, sb_=sb_: e.activation(out=sg[sb_][:], in_=pg[:], func=AF.Silu),
                          reads=[("psM", j % 2)], writes=[("sg", sb_)])
                    S.add("dve", lambda e, pu=pu, sb_=sb_, j=j: e.tensor_tensor(
                        out=act[:, j, :], in0=pu[:], in1=sg[sb_][:], op=ALU.mult),
                        reads=[("psM", 2 + j % 2), ("sg", sb_)], writes=[("act", j)])
                for c in range(4):
                    cg = tt * 4 + c
                    xb = cg % 2
                    S.add("sp", lambda e, xb=xb, cg=cg: e.dma_start(out=x1t[xb][:], in_=x1d[cg * 128:(cg + 1) * 128, :]),
                          writes=[("x1t", xb)], dma=True)
                    for half in range(2):
                        pd = psM[4 + half]
                        for j in range(NJ):
                            S.add("pe", mm(pd[:], act[:, j, c * 128:(c + 1) * 128], Wd[:, j, half * 512:(half + 1) * 512],
                                           j == 0, j == NJ - 1), reads=[("act", j), ("Wd", j // 2)], writes=[("psM", 4 + half)])
                        S.add("dve", lambda e, pd=pd, xb=xb, half=half: e.tensor_tensor(
                            out=x2[xb][:, half * 512:(half + 1) * 512], in0=pd[:],
                            in1=x1t[xb][:, half * 512:(half + 1) * 512], op=ALU.add),
                            reads=[("psM", 4 + half), ("x1t", xb)], writes=[("x1t", xb)])
                    rmsnorm_chunk(x2[xb][:], ("x1t", xb), gF[:], yo[xb][:], ("x1t", xb), ssE, sdE, rsE, cg, "E")
                    S.add("sp", lambda e, xb=xb, cg=cg: e.dma_start(out=out[cg * 128:(cg + 1) * 128, :], in_=yo[xb][:]),
                          reads=[("x1t", xb)], writes=["outd"], dma=True)
            S.barrier()
            S.emit()
    return nc


def _consts():
    pos = np.arange(S_LEN, dtype=np.float64)
    inv = 10000.0 ** (-np.arange(0, 64, 2, dtype=np.float64) / 64.0)
    p = np.arange(128)
    angA = pos[None, :] * inv[(p % 64) % 32][:, None]
    cosA = np.cos(angA).astype(np.float32)
    sinA = np.sin(angA).astype(np.float32)
    base = 1.0 / (10000.0 ** np.linspace(0.0, 1.0, 64, dtype=np.float64))
    angR = pos[None, :] * base[p // 2][:, None]
    cosR = np.cos(angR).astype(np.float32)
    sinR = np.sin(angR).astype(np.float32)
    cb = np.zeros((128, CB_W), np.float32)
    cb[:, CB_ID:CB_ID + 128] = np.eye(128)
    for b in range(2):
        for m in range(64):
            if m < 32:
                cb[64 * b + m + 32, CB_ROTA + 64 * b + m] = -1.0
            else:
                cb[64 * b + m - 32, CB_ROTA + 64 * b + m] = 1.0
    for i in range(64):
        cb[2 * i + 1, CB_ROTR + 2 * i] = -1.0
        cb[2 * i, CB_ROTR + 2 * i + 1] = 1.0
    jj = np.arange(128)[:, None]
    qq = np.arange(128)[None, :]
    cb[:, CB_MASK:CB_MASK + 128] = np.where(jj <= qq, 0.0, NEG)
    cb[:, CB_MASK + 128:CB_MASK + 256] = np.where(jj >= qq, 0.0, NEG)
    cb[:, CB_ONE0:CB_ONE0 + 64] = 1.0
    cb[:, CB_ONE1 + 64:CB_ONE1 + 128] = 1.0
    cf = np.zeros((128, CF_W), np.float32)
    for h in range(4):
        lg = np.log1p(-(2.0 ** (-5.0 - h)))
        diff = (qq - jj).astype(np.float64)
        dec = np.where(diff >= 0, np.exp(np.maximum(diff, 0.0) * lg), 0.0)
        cf[:, CF_DEC + h * 128:CF_DEC + (h + 1) * 128] = dec
        cf[:, CF_ZETA + h] = np.exp((127 - np.arange(128)) * lg)
        cf[:, CF_XI + h * 128:CF_XI + (h + 1) * 128] = np.exp((np.arange(128) + 1.0) * lg)[None, :]
    return dict(cosA=cosA, sinA=sinA, cosR=cosR, sinR=sinR, cb=cb, cf=cf)


_CACHE = {}


def _run(inputs, debug=None):
    key = tuple(sorted(debug)) if debug else None
    if key not in _CACHE:
        _CACHE[key] = build_program(debug)
    nc = _CACHE[key]
    f = lambda a: np.ascontiguousarray(np.asarray(a, dtype=np.float32))
    x = f(inputs["x"])
    consts = _consts()
    shared = dict(
        w_in=f(inputs["w_in"])[0], w_out_attn=f(inputs["w_out_attn"])[0], w_out_ret=f(inputs["w_out_ret"])[0],
        w_out=f(inputs["w_out"])[0], w_ffn_gate=f(inputs["w_ffn_gate"])[0], w_ffn_up=f(inputs["w_ffn_up"])[0],
        w_ffn_down=f(inputs["w_ffn_down"])[0],
        g1T=np.ascontiguousarray(f(inputs["norm_mix_g"])[0].reshape(8, 128).T),
        g2rep=np.ascontiguousarray(np.broadcast_to(f(inputs["norm_ffn_g"])[0][None, :], (128, D))),
        gFrep=np.ascontiguousarray(np.broadcast_to(f(inputs["norm_final_g"])[None, :], (128, D))),
        **consts)
    in_maps = [dict(shared, x=np.ascontiguousarray(x[b])) for b in range(8)]
    res = run_bass_kernel_spmd(nc, in_maps, core_ids=list(range(8)))
    return res


def kernel(**inputs):
    res = _run(inputs)
    return np.stack([np.asarray(res.results[b]["out"], dtype=np.float32) for b in range(8)], axis=0)
```
